# Optimizing a Trainium2 kernel written in Bass

```python
import math
import jax, jax.numpy as jnp
from jax import lax
import numpy as np

D_MODEL = 1024
BATCH = 16
SEQ = 2048
DEPTH = 4

HEAD_DIM = 64
N_MIXERS = 4
GROUP_WIDTH = D_MODEL // N_MIXERS
GROUP_HEADS = GROUP_WIDTH // HEAD_DIM
CHUNK = 128
POOL_WINDOWS = (2, 4, 8, 16)
POOL_GROUPS = len(POOL_WINDOWS)
POOL_GROUP_DIM = GROUP_WIDTH // POOL_GROUPS
WINDOW = 128
SWA_Q_HEADS = GROUP_HEADS
SWA_KV_HEADS = 2
SWA_GROUP = SWA_Q_HEADS // SWA_KV_HEADS
SB_HEADS = GROUP_HEADS
SB_BLOCK = 128
N_BUCKETS = 32
MAX_DISTANCE = 128
D_FF = ((8 * D_MODEL // 3 + 255) // 256) * 256
EPS = 1e-6

IN_SIZES = (GROUP_WIDTH, GROUP_WIDTH, GROUP_WIDTH,
            GROUP_WIDTH, SWA_KV_HEADS * HEAD_DIM, SWA_KV_HEADS * HEAD_DIM,
            GROUP_WIDTH, GROUP_WIDTH, GROUP_WIDTH)
D_IN = sum(IN_SIZES)

kernel_name = "hybrid_parallel_headgroup_trunk"


def _split_points():
    pts, acc = [], 0
    for s in IN_SIZES[:-1]:
        acc += s
        pts.append(acc)
    return pts


def _rmsnorm(x, g):
    xf = x.astype(jnp.float32)
    y = xf * lax.rsqrt(jnp.mean(xf * xf, axis=-1, keepdims=True) + EPS)
    return (y * g.astype(jnp.float32)).astype(x.dtype)


def _layernorm_noaffine(x):
    xf = x.astype(jnp.float32)
    mu = jnp.mean(xf, axis=-1, keepdims=True)
    xc = xf - mu
    y = xc * lax.rsqrt(jnp.mean(xc * xc, axis=-1, keepdims=True) + EPS)
    return y.astype(x.dtype)


def _chunked_sgu(u, v, w_s, b_s):
    B, S, _ = u.shape
    nc = S // CHUNK
    u = jax.nn.gelu(u)
    v = _layernorm_noaffine(jax.nn.gelu(v).reshape(B, nc, CHUNK, GROUP_HEADS, HEAD_DIM))
    causal = jnp.tril(jnp.ones((CHUNK, CHUNK), w_s.dtype))
    w = w_s * causal[None]
    mix = jnp.einsum('hts,bnshd->bnthd', w, v) + b_s.T[None, None, :, :, None]
    return u * mix.reshape(B, S, GROUP_WIDTH)


def _multiscale_pool(p, w_pool, scale):
    B, S, _ = p.shape
    pg = p.reshape(B, S, POOL_GROUPS, POOL_GROUP_DIM)
    csum = jnp.cumsum(pg.astype(jnp.float32), axis=1)
    csum = jnp.pad(csum, ((0, 0), (1, 0), (0, 0), (0, 0)))
    t = jnp.arange(S)[:, None]
    win = jnp.array(POOL_WINDOWS, jnp.int32)[None, :]
    start = jnp.maximum(t + 1 - win, 0)
    gidx = jnp.arange(POOL_GROUPS)[None, :]
    window_sum = csum[:, 1:] - csum[:, start, gidx]
    count = (t + 1 - start).astype(jnp.float32)
    pooled = window_sum / count[None, :, :, None]
    y = pooled.astype(p.dtype) - pg
    y = jnp.einsum('bsgc,gcd->bsgd', y, w_pool)
    return y.reshape(B, S, GROUP_WIDTH) * scale


def _t5_bucket(dist):
    max_exact = N_BUCKETS // 2
    df = jnp.maximum(dist, 1).astype(jnp.float32)
    large = max_exact + (jnp.log(df / max_exact) / math.log(MAX_DISTANCE / max_exact)
                         * (N_BUCKETS - max_exact)).astype(jnp.int32)
    large = jnp.minimum(large, N_BUCKETS - 1)
    return jnp.where(dist < max_exact, dist, large)


def _swa_sink_attention(q, k, v, sinks, rel_bias):
    B, S, _ = q.shape
    nb = S // WINDOW
    qb = q.reshape(B, nb, WINDOW, SWA_KV_HEADS, SWA_GROUP, HEAD_DIM)
    kb = k.reshape(B, nb, WINDOW, SWA_KV_HEADS, HEAD_DIM)
    vb = v.reshape(B, nb, WINDOW, SWA_KV_HEADS, HEAD_DIM)
    pad = ((0, 0), (1, 0), (0, 0), (0, 0), (0, 0))
    k2 = jnp.concatenate([jnp.pad(kb, pad)[:, :-1], kb], axis=2)
    v2 = jnp.concatenate([jnp.pad(vb, pad)[:, :-1], vb], axis=2)
    logits = jnp.einsum('bnqhgd,bnkhd->bnhgqk', qb, k2,
                        preferred_element_type=jnp.float32) * (HEAD_DIM ** -0.5)
    dist = (jnp.arange(WINDOW)[:, None] + WINDOW) - jnp.arange(2 * WINDOW)[None, :]
    in_window = (dist >= 0) & (dist < WINDOW)
    bias = rel_bias.astype(jnp.float32)[_t5_bucket(jnp.clip(dist, 0, WINDOW - 1))]
    bias = bias.transpose(2, 0, 1).reshape(SWA_KV_HEADS, SWA_GROUP, WINDOW, 2 * WINDOW)
    not_pad = (jnp.arange(nb)[:, None] > 0) | (jnp.arange(2 * WINDOW)[None, :] >= WINDOW)
    mask = in_window[None] & not_pad[:, None, :]
    logits = jnp.where(mask[None, :, None, None], logits + bias, -1e30)
    sink = jnp.broadcast_to(sinks.astype(jnp.float32).reshape(SWA_KV_HEADS, SWA_GROUP, 1, 1),
                            logits.shape[:-1] + (1,))
    probs = jax.nn.softmax(jnp.concatenate([logits, sink], axis=-1), axis=-1)[..., :-1]
    out = jnp.einsum('bnhgqk,bnkhd->bnqhgd', probs.astype(v.dtype), v2)
    return out.reshape(B, S, GROUP_WIDTH)


def _stick_breaking_attention(q, k, v):
    B, S, _ = q.shape
    nb = S // SB_BLOCK
    kh = k.reshape(B, S, SB_HEADS, HEAD_DIM)
    vh = v.reshape(B, S, SB_HEADS, HEAD_DIM)
    qb = q.reshape(B, nb, SB_BLOCK, SB_HEADS, HEAD_DIM).transpose(1, 0, 2, 3, 4)
    key_pos = jnp.arange(S)
    scale = HEAD_DIM ** -0.5

    def block(args):
        qblk, n = args
        z = jnp.einsum('bqhd,bkhd->bhqk', qblk, kh,
                       preferred_element_type=jnp.float32) * scale
        q_pos = n * SB_BLOCK + jnp.arange(SB_BLOCK)
        causal = key_pos[None, :] < q_pos[:, None]
        log_1m = jnp.where(causal, jax.nn.log_sigmoid(-z), 0.0)
        tail = lax.cumsum(log_1m, axis=3, reverse=True) - log_1m
        w = jnp.where(causal, jnp.exp(jax.nn.log_sigmoid(z) + tail), 0.0)
        return jnp.einsum('bhqk,bkhd->bqhd', w.astype(v.dtype), vh)

    out = lax.map(block, (qb, jnp.arange(nb)))
    return out.transpose(1, 0, 2, 3, 4).reshape(B, S, GROUP_WIDTH)


def setup_inputs(seed: int = 0) -> dict:
    key = jax.random.key(seed)
    ks = jax.random.split(key, 16)
    f32 = jnp.float32
    nrm = lambda k, shape, s: jax.random.normal(k, shape, f32) * s
    return {
        "x": jax.random.normal(ks[0], (BATCH, SEQ, D_MODEL), f32),
        "w_in": nrm(ks[1], (DEPTH, D_MODEL, D_IN), D_MODEL ** -0.5),
        "w_out": nrm(ks[2], (DEPTH, D_MODEL, D_MODEL), D_MODEL ** -0.5),
        "sgu_w": nrm(ks[3], (DEPTH, GROUP_HEADS, CHUNK, CHUNK), CHUNK ** -0.5),
        "sgu_b": 1.0 + nrm(ks[4], (DEPTH, GROUP_HEADS, CHUNK), 0.02),
        "pool_w": nrm(ks[5], (DEPTH, POOL_GROUPS, POOL_GROUP_DIM, POOL_GROUP_DIM), POOL_GROUP_DIM ** -0.5),
        "pool_scale": 1.0 + nrm(ks[6], (DEPTH, GROUP_WIDTH), 0.02),
        "swa_sinks": nrm(ks[7], (DEPTH, SWA_Q_HEADS), 1.0),
        "rel_bias": nrm(ks[8], (N_BUCKETS, SWA_Q_HEADS), 0.5),
        "mix_out_gain": 1.0 + nrm(ks[9], (DEPTH, D_MODEL), 0.02),
        "norm_mix": 1.0 + nrm(ks[10], (DEPTH, D_MODEL), 0.02),
        "norm_ffn": 1.0 + nrm(ks[11], (DEPTH, D_MODEL), 0.02),
        "w_gate_up": nrm(ks[12], (DEPTH, D_MODEL, 2 * D_FF), D_MODEL ** -0.5),
        "w_down": nrm(ks[13], (DEPTH, D_FF, D_MODEL), D_FF ** -0.5),
        "norm_final": 1.0 + nrm(ks[14], (D_MODEL,), 0.02),
    }


def reference(x, w_in, w_out, sgu_w, sgu_b, pool_w, pool_scale, swa_sinks, rel_bias,
              mix_out_gain, norm_mix, norm_ffn, w_gate_up, w_down, norm_final):
    B, S, _ = x.shape
    splits = _split_points()
    for l in range(DEPTH):
        h = _rmsnorm(x, norm_mix[l])
        proj = h @ w_in[l]
        a_u, a_v, b_in, c_q, c_k, c_v, d_q, d_k, d_v = jnp.split(proj, splits, axis=-1)
        y_a = _chunked_sgu(a_u, a_v, sgu_w[l], sgu_b[l])
        y_b = _multiscale_pool(b_in, pool_w[l], pool_scale[l])
        y_c = _swa_sink_attention(c_q, c_k, c_v, swa_sinks[l], rel_bias)
        y_d = _stick_breaking_attention(d_q, d_k, d_v)
        ycat = jnp.stack([y_a, y_b, y_c, y_d], axis=2)
        ycat = _rmsnorm(ycat, mix_out_gain[l].reshape(N_MIXERS, GROUP_WIDTH))
        x = x + ycat.reshape(B, S, D_MODEL) @ w_out[l]
        h = _rmsnorm(x, norm_ffn[l])
        gate, up = jnp.split(h @ w_gate_up[l], 2, axis=-1)
        x = x + (jax.nn.silu(gate) * up) @ w_down[l]
    return _rmsnorm(x, norm_final)
```

```python
import math
from contextlib import ExitStack

import numpy as np
import ml_dtypes
import concourse.bass as bass
import concourse.mybir as mybir
from concourse.bass_utils import run_bass_kernel_spmd

F32 = mybir.dt.float32
BF16 = mybir.dt.bfloat16
ALU = mybir.AluOpType
AF = mybir.ActivationFunctionType
AX = mybir.AxisListType

D = 1024
T = 2048
NT = 16
DFF = 2816
EPS = 1e-6
NEG = -30000.0
POOLW = (2, 4, 8, 16)
NP_LP = 1056
NB_C = 2176
NF_C = 1796
SEM_CH = 30000


class Sched:
    ENG = ("pe", "act", "dve", "pool", "sp")

    def __init__(self):
        self.ops = {e: [] for e in self.ENG}
        self.nops = {e: 0 for e in self.ENG}
        self.clock = {e: {} for e in self.ENG}
        self.lastw = {}
        self.readers = {}
        self.dma_cnt = {}
        self.waited_on = {e: set() for e in self.ENG}
        self.serialize = False
        self.last_sig = None
        self._cap = None

    def begin_capture(self):
        self._cap = []

    def end_capture(self):
        c = self._cap
        self._cap = None
        return c

    def play(self, lists):
        pos = [0] * len(lists)
        while True:
            best, bf = None, None
            for i, l in enumerate(lists):
                if pos[i] < len(l):
                    f = (pos[i] + 0.5) / len(l)
                    if bf is None or f < bf:
                        best, bf = i, f
            if best is None:
                break
            self.op(*lists[best][pos[best]])
            pos[best] += 1

    def op(self, eng, fn, reads=(), writes=(), dma=None):
        if self._cap is not None:
            self._cap.append((eng, fn, tuple(reads), tuple(writes), dma))
            return None
        raw, other = [], []
        if self.serialize and self.last_sig is not None:
            other.append(self.last_sig)
        for r in reads:
            s = self.lastw.get(r)
            if s is not None:
                raw.append(s)
            if r.startswith("ps"):
                other.extend(self.readers.get(r, ()))
        for w in writes:
            s = self.lastw.get(w)
            if s is not None:
                other.append(s)
            other.extend(self.readers.get(w, ()))
        clk = self.clock[eng]
        need = {}
        for is_raw, lst in ((True, raw), (False, other)):
            for (k, v, c) in lst:
                if clk.get(k, 0) >= v:
                    continue
                if k == eng:
                    if eng == "pe":
                        continue
                if k not in need or need[k][0] < v:
                    need[k] = (v, c)
        waits = []
        for k, (v, c) in need.items():
            if clk.get(k, 0) >= v:
                continue
            waits.append((k, v))
            for kk, vv in c.items():
                if clk.get(kk, 0) < vv:
                    clk[kk] = vv
            clk[k] = v
            if k in self.ENG:
                self.waited_on[k].add(v)
        if dma is None:
            self.nops[eng] += 1
            idx = self.nops[eng]
            sc = dict(clk)
            sc[eng] = idx
            sig = (eng, idx, sc)
            rec = dict(fn=fn, waits=waits, idx=idx, dma=None)
        else:
            dkey, dn = dma
            self.dma_cnt[dkey] = self.dma_cnt.get(dkey, 0) + dn
            v = self.dma_cnt[dkey]
            sc = dict(clk)
            sc[dkey] = v
            sig = (dkey, v, sc)
            rec = dict(fn=fn, waits=waits, idx=None, dma=dkey, dn=dn)
        self.ops[eng].append(rec)
        if fn is not None:
            self.last_sig = sig
        for r in reads:
            self.readers.setdefault(r, []).append(sig)
        for w in writes:
            self.lastw[w] = sig
            self.readers[w] = []
        return sig

    def barrier(self, engs=("pe", "act", "dve")):
        sigs = {}
        for e in engs:
            if self.nops[e] > 0:
                sc = dict(self.clock[e])
                sc[e] = self.nops[e]
                sigs[e] = (self.nops[e], sc)
        for e in engs:
            waits = []
            clk = self.clock[e]
            for k, (v, c) in sigs.items():
                if k == e or clk.get(k, 0) >= v:
                    continue
                waits.append((k, v))
                for kk, vv in c.items():
                    if clk.get(kk, 0) < vv:
                        clk[kk] = vv
                clk[k] = v
                self.waited_on[k].add(v)
            if waits:
                self.ops[e].append(dict(fn=None, waits=waits, idx=None, dma=None))

    def emit(self, nc, stack):
        valmap = {}
        nsem = {}
        for e in self.ENG:
            m = {}
            c = 0
            for i in range(1, self.nops[e] + 1):
                if i in self.waited_on[e]:
                    m[i] = (c // SEM_CH, c % SEM_CH + 1)
                    c += 1
            valmap[e] = m
            nsem[e] = c // SEM_CH + 1
        sems = {}
        for e in self.ENG:
            sems[e] = [stack.enter_context(nc.semaphore("s_%s%d" % (e, j))) for j in range(nsem[e])]
        for k in self.dma_cnt:
            sems[k] = stack.enter_context(nc.semaphore("d_" + str(k)))
        self.n_signals = {e: len(valmap[e]) for e in self.ENG}
        block = stack.enter_context(nc.Block())

        def mk(e):
            def run(engobj):
                for rec in self.ops[e]:
                    for (k, v) in rec["waits"]:
                        if k in self.ENG:
                            ep, val = valmap[k][v]
                            engobj.wait_ge(sems[k][ep], val)
                        else:
                            engobj.wait_ge(sems[k], 16 * v)
                    if rec["fn"] is None:
                        continue
                    ins = rec["fn"](engobj)
                    if rec["dma"] is not None:
                        assert len(ins) == rec["dn"]
                        for i_ in ins:
                            i_.then_inc(sems[rec["dma"]], 16)
                    elif rec["idx"] in valmap[e]:
                        ins.then_inc(sems[e][valmap[e][rec["idx"]][0]], 1)
            return run

        block.tensor(mk("pe"))
        block.scalar(mk("act"))
        block.vector(mk("dve"))
        block.gpsimd(mk("pool"))
        block.sync(mk("sp"))


def seq(fs):
    def run(e):
        ins = None
        for f in fs:
            ins = f(e)
        return ins
    return run


def MM(out, lhsT, rhs, start=True, stop=True, skip=False):
    if skip:
        return lambda e: e.matmul(out, lhsT=lhsT, rhs=rhs, start=start, stop=stop, skip_group_check=True)
    return lambda e: e.matmul(out, lhsT=lhsT, rhs=rhs, start=start, stop=stop)


def TRN(out, in_, ident):
    return lambda e: e.transpose(out, in_, ident)


def ACTV(out, in_, func, bias=None, scale=None, accum_out=None):
    kw = {}
    if bias is not None:
        kw["bias"] = bias
    if scale is not None:
        kw["scale"] = scale
    if accum_out is not None:
        kw["accum_out"] = accum_out
    return lambda e: e.activation(out=out, in_=in_, func=func, **kw)


def TT(out, in0, in1, op):
    return lambda e: e.tensor_tensor(out=out, in0=in0, in1=in1, op=op)


def TS(out, in0, s1, op0, s2=None, op1=None):
    if op1 is None:
        return lambda e: e.tensor_scalar(out=out, in0=in0, scalar1=s1, scalar2=None, op0=op0)
    return lambda e: e.tensor_scalar(out=out, in0=in0, scalar1=s1, scalar2=s2, op0=op0, op1=op1)


def STT(out, in0, scalar, in1, op0, op1):
    return lambda e: e.scalar_tensor_tensor(out=out, in0=in0, scalar=scalar, in1=in1, op0=op0, op1=op1)


def RED(out, in_, op):
    return lambda e: e.tensor_reduce(out=out, in_=in_, axis=AX.X, op=op)


def CP(out, in_):
    return lambda e: e.tensor_copy(out=out, in_=in_)


def DMA(out, in_):
    return lambda e: [e.dma_start(out=out, in_=in_)]


def DMAS(pairs):
    return lambda e: [e.dma_start(out=o, in_=i) for (o, i) in pairs]


class _Cut(Exception):
    pass


def build_program(NL=4, NS=2, dbg=None):
    nc = bass.Bass("TRN2", target_bir_lowering=False)

    ckcnt = {}

    def ck(k):
        if dbg is None:
            return
        kk, occ = dbg if isinstance(dbg, tuple) else (dbg, 1)
        if kk == k:
            ckcnt[k] = ckcnt.get(k, 0) + 1
            if ckcnt[k] == occ:
                raise _Cut()

    x_d = nc.dram_tensor("x", [NS, T, D], F32, kind="ExternalInput").ap()
    win_d = nc.dram_tensor("w_in", [NL, D, 2048], F32, kind="ExternalInput").ap()
    wout_d = nc.dram_tensor("w_out", [NL, D, D], F32, kind="ExternalInput").ap()
    wgu_d = nc.dram_tensor("w_gu", [NL, D, 2 * DFF], F32, kind="ExternalInput").ap()
    wdn_d = nc.dram_tensor("w_dn", [NL, DFF, D], F32, kind="ExternalInput").ap()
    lp_d = nc.dram_tensor("lp", [NL, 128, NP_LP], F32, kind="ExternalInput").ap()
    cbf_d = nc.dram_tensor("cbf", [128, NB_C], BF16, kind="ExternalInput").ap()
    c32_d = nc.dram_tensor("c32", [128, NF_C], F32, kind="ExternalInput").ap()
    gf_d = nc.dram_tensor("gf", [128, D], F32, kind="ExternalInput").ap()
    y_d = nc.dram_tensor("y", [NS, T, D], F32, kind="ExternalOutput").ap()

    S = Sched()
    import os as _os
    S.serialize = bool(_os.environ.get("KSERIAL"))
    base = [16512]
    LIMIT = 229376

    def sb(name, shape, dt, at=None):
        nb = int(np.prod(shape[1:])) * (4 if dt == F32 else 2)
        nb = (nb + 63) // 64 * 64
        if at is None:
            off = base[0]
            base[0] += nb
        else:
            off = at
        assert off + nb <= LIMIT, (name, off, nb)
        return nc.alloc_sbuf_tensor_at(name, list(shape), dt, offset=off), off + nb

    x, _ = sb("xres", [128, NT, D], F32)
    WS, WSD = [], []
    for i in range(6):
        off = base[0]
        h, _ = sb("ws%d" % i, [128, 8, 512], BF16)
        WS.append(h)
        h2, _ = sb("wsd%d" % i, [128, 4, 1024], BF16, at=off)
        WSD.append(h2)
    kTc, _ = sb("kTc", [128, T], BF16)
    kTd, _ = sb("kTd", [128, 2, T], BF16)
    vc, _ = sb("vc", [128, NT, 128], BF16)
    vd, _ = sb("vd", [128, NT, 256], BF16)
    cbf, _ = sb("cbf", [128, NB_C], BF16)
    c32, _ = sb("c32", [128, NF_C], F32)
    lp, _ = sb("lp", [128, NP_LP], F32)
    WmT, _ = sb("WmT", [128, 4, 128], BF16)
    wp, _ = sb("wp", [64, 256], BF16)
    st, _ = sb("stats", [128, 64], F32)
    zb, _ = sb("zb", [128, 512], BF16)
    ubase = base[0]

    hT, _ = sb("hT", [128, 8, 512], BF16)
    hn, _ = sb("hn", [128, D], BF16)
    qTc, _ = sb("qTc", [128, 2, 512], BF16)
    qTd, _ = sb("qTd", [128, 2, 512], BF16)
    uv = [sb("uv%d" % i, [128, 512], F32)[0] for i in range(2)]
    bin_ = [sb("bin%d" % i, [128, 256], BF16)[0] for i in range(5)]
    vn = [sb("vn%d" % i, [128, 256], BF16)[0] for i in range(2)]
    sq, _ = sb("sq", [128, 256], F32)
    ytmp = [sb("ytmp%d" % i, [128, 256], F32)[0] for i in range(2)]
    ycat, _ = sb("ycat", [128, 4, D], BF16)
    yT, _ = sb("yT", [64, 4, 128], BF16)
    swb, _ = sb("swb", [128, 4, 256], F32)
    Pb, _ = sb("Pb", [128, 4, 256], BF16)
    PT, _ = sb("PT", [128, 8, 128], BF16)
    ycT = PT
    ysb, _ = sb("ysb", [128, 4, 256], F32)
    Eb, _ = sb("Eb", [128, 512], F32)
    Lp = [sb("Lp%d" % i, [128, 512], BF16)[0] for i in range(2)]
    wTb = [sb("wT%d" % i, [128, 512], BF16)[0] for i in range(2)]
    Rb = [sb("R%d" % i, [128, 512], BF16)[0] for i in range(2)]
    m_end = base[0]
    base[0] = ubase
    gFb, _ = sb("gFb", [128, D], F32, at=ubase)
    h2T, _ = sb("h2T", [128, 8, T], BF16)
    hn2, _ = sb("hn2", [128, D], BF16)
    sg = [sb("sg%d" % i, [128, 512], F32)[0] for i in range(2)]
    aT = [sb("aT%d" % i, [128, 4, 512], BF16)[0] for i in range(2)]
    f_end = base[0]
    sbuf_used = max(m_end, f_end)
    assert sbuf_used <= LIMIT

    ps = [nc.alloc_psum_tensor("ps%d" % i, [128, 512], F32) for i in range(8)]
    psb = [p.bitcast(BF16) for p in ps]
    POOLS = {"all": [0, 1, 2, 3, 4, 5, 7], "sb": [0, 1, 2], "ma": [3], "mbc": [5, 7]}
    rot = {"all": 0, "sb": 0, "ma": 0, "mbc": 0}
    bmode = ["all"]

    def nb():
        m = bmode[0]
        i = POOLS[m][rot[m]]
        rot[m] = (rot[m] + 1) % len(POOLS[m])
        return i

    ident = cbf[:, 0:128]
    negtri = cbf[:, 128:256]
    negones = cbf[:, 256:384]
    maskb = cbf[:, 384:512]
    mask01T = cbf[:, 512:640]

    def band(wi, kind):
        o = 640 + 128 * (wi * 3 + kind)
        return cbf[:, o:o + 128]

    bias_tab = c32[:, 0:1024].rearrange("p (h k) -> p h k", h=4)
    invc_first = c32[0:64, 1280:1792].rearrange("p (g t) -> p g t", g=4)
    wsc = c32[0:64, 1792:1796]
    g1 = lp[:, 0:8]
    g2 = lp[:, 8:16]
    g3 = lp[:, 16:24]
    sgub = lp[:, 24:28]
    sinks = lp[:, 28:32]

    S.op("sp", DMA(cbf[:], cbf_d), writes=["cbf"], dma=("dc0", 1))
    S.op("sp", DMA(c32[:], c32_d), writes=["c32"], dma=("dc1", 1))
    S.op("dve", TT(bias_tab, bias_tab, c32[:, 1024:1280].unsqueeze(1).broadcast_to([128, 4, 256]), ALU.add),
         reads=["c32"], writes=["c32"])

    S.op("dve", lambda e: e.memset(zb[:], 0.0), writes=["zb"])

    def keep_warm():
        S.op("pe", MM(ps[4][:, :], zb[:, 0:128], zb[:, :], start=True, stop=True), reads=["zb"], writes=["ps4"])

    def rstd_from_ss(col, n_inv):
        S.op("act", ACTV(st[:, col:col + 1], st[:, col:col + 1], AF.Ln, bias=EPS, scale=n_inv),
             reads=["st%d" % col], writes=["st%d" % col])
        S.op("act", ACTV(st[:, col:col + 1], st[:, col:col + 1], AF.Exp, scale=-0.5),
             reads=["st%d" % col], writes=["st%d" % col])

    def norm_transpose(src, srckeys, hnbuf, hnkey, gain, dst_full, dstkeys, col, junk, junkkey):
        S.op("act", ACTV(junk, src, AF.Square, accum_out=st[:, col:col + 1]),
             reads=srckeys, writes=[junkkey, "st%d" % col])
        rstd_from_ss(col, 1.0 / D)
        S.op("act", ACTV(hnbuf[:], src, AF.Copy, scale=st[:, col:col + 1]),
             reads=srckeys + ["st%d" % col], writes=[hnkey])
        b = nb()
        S.op("pe", seq([TRN(psb[b][:, j * 128:(j + 1) * 128], hnbuf[:, j * 128:(j + 1) * 128], ident) for j in range(8)]),
             reads=[hnkey, "cbf"], writes=["ps%d" % b])
        S.op("dve", TT(dst_full(), psb[b][:, :].rearrange("p (c t) -> p c t", c=8),
                       gain[:, 0:8].unsqueeze(2).broadcast_to([128, 8, 128]), ALU.mult),
             reads=["ps%d" % b, "lp"], writes=dstkeys)

    def load_w(slot, dst, src):
        S.op("pool", DMA(dst, src), writes=["ws%d" % slot], dma=("dw%d" % slot, 1))

    def layer(s, l, first, last_layer):
        S.op("sp", DMA(lp[:], lp_d[l]), writes=["lp"], dma=("dlp", 1))
        S.op("dve", TT(WmT[:], lp[:, 288:800].rearrange("p (h t) -> p h t", h=4),
                       mask01T.unsqueeze(1).broadcast_to([128, 4, 128]), ALU.mult),
             reads=["lp", "cbf"], writes=["WmT"])
        S.op("dve", TT(wp[:], lp[0:64, 800:1056], lp[0:64, 32:288], ALU.mult), reads=["lp"], writes=["wp"])
        win_v = win_d[l].rearrange("(kc p) n -> p kc n", p=128)
        wout_v = wout_d[l].rearrange("(kc p) n -> p kc n", p=128)
        for q in range(4):
            load_w(q, WS[q][:], win_v[:, :, q * 512:(q + 1) * 512])
        for q in range(2):
            load_w(4 + q, WS[4 + q][:], wout_v[:, :, q * 512:(q + 1) * 512])

        def win(kc, c0, c1):
            q = c0 // 512
            assert (c1 - 1) // 512 == q
            return WS[q][:, kc, c0 - q * 512:c1 - q * 512], "ws%d" % q

        ck(1)
        for g in range(4):
            for ti in range(4):
                n = 4 * g + ti
                norm_transpose(x[:, n, :], ["x%d" % n], hn, "hn", g1,
                               lambda ti=ti: hT[:, :, ti * 128:(ti + 1) * 128],
                               ["hT%d" % ti], col=0, junk=Pb[:].rearrange("p h k -> p (h k)"), junkkey="Pb")
            hTkeys = ["hT%d" % i for i in range(4)]
            ck(2)
            for fc in range(7):
                c0 = 1152 + fc * 128
                b = nb()
                fs = []
                wk = None
                for kc in range(8):
                    wap, wk = win(kc, c0, c0 + 128)
                    fs.append(MM(ps[b][:, :], wap, hT[:, kc, :], start=(kc == 0), stop=(kc == 7)))
                S.op("pe", seq(fs), reads=hTkeys + [wk], writes=["ps%d" % b])
                if fc < 2:
                    S.op("act", ACTV(qTc[:, fc, :], ps[b][:, :], AF.Copy, scale=0.125), reads=["ps%d" % b], writes=["qTc"])
                elif fc == 2:
                    S.op("act", ACTV(kTc[:, g * 512:(g + 1) * 512], ps[b][:, :], AF.Copy), reads=["ps%d" % b], writes=["kTc"])
                elif fc < 5:
                    S.op("act", ACTV(qTd[:, fc - 3, :], ps[b][:, :], AF.Copy, scale=0.125), reads=["ps%d" % b], writes=["qTd"])
                else:
                    S.op("dve", CP(kTd[:, fc - 5, g * 512:(g + 1) * 512], ps[b][:, :]), reads=["ps%d" % b], writes=["kTd"])
            ck(3)
            for ti in range(4):
                n = 4 * g + ti
                b = nb()
                S.op("pe", seq([MM(ps[b][:, :], hT[:, kc, ti * 128:(ti + 1) * 128], WS[1][:, kc, :], start=(kc == 0), stop=(kc == 7))
                                for kc in range(8)]), reads=["hT%d" % ti, "ws1"], writes=["ps%d" % b])
                S.op("dve", CP(bin_[n % 5][:], ps[b][:, 0:256]), reads=["ps%d" % b], writes=["bin%d" % (n % 5)])
                S.op("dve", CP(vd[:, n, :], ps[b][:, 256:512]), reads=["ps%d" % b], writes=["vd"])
                ck(312)
                b = nb()
                S.op("pe", seq([MM(ps[b][:, 0:128], hT[:, kc, ti * 128:(ti + 1) * 128], WS[2][:, kc, 0:128], start=(kc == 0), stop=(kc == 7))
                                for kc in range(8)]), reads=["hT%d" % ti, "ws2"], writes=["ps%d" % b])
                S.op("dve", CP(vc[:, n, :], ps[b][:, 0:128]), reads=["ps%d" % b], writes=["vc"])

            S.begin_capture()
            bmode[0] = "sb"
            stick_breaking(g)
            chain_sb = S.end_capture()
            S.begin_capture()
            bmode[0] = "ma"
            for ti in range(4):
                n = 4 * g + ti
                uvb = uv[n % 2]
                uvk = "uv%d" % (n % 2)
                if _os.environ.get("KVAR", "") == "4" and n > 0:
                    nb()
                b = nb()
                S.op("pe", seq([MM(ps[b][:, :], hT[:, kc, ti * 128:(ti + 1) * 128], WS[0][:, kc, :], start=(kc == 0), stop=(kc == 7))
                                for kc in range(8)]), reads=["hT%d" % ti, "ws0"], writes=["ps%d" % b])
                _var = _os.environ.get("KVAR", "")
                if _var == "1" and n > 0:
                    S.op("act", ACTV(uvb[:], ps[b][:, :], AF.Copy), reads=["ps%d" % b], writes=[uvk])
                elif _var == "2" and n > 0:
                    S.op("act", ACTV(uv[0][:], ps[b][:, :], AF.Gelu_apprx_tanh), reads=["ps%d" % b], writes=["uv0"])
                elif _var == "3" and n > 0:
                    S.op("dve", CP(uvb[:], ps[b][:, :]), reads=["ps%d" % b], writes=[uvk])
                else:
                    pa = ps[b][:, :]
                    S.op("dve", CP(uvb[:], pa), reads=["ps%d" % b], writes=[uvk])
                    S.op("dve", TT(uvb[:], uvb[:], uvb[:], ALU.mult), reads=[uvk], writes=[uvk])
                    S.op("dve", TS(uvb[:], uvb[:], 0.044715, ALU.mult, 1.0, ALU.add), reads=[uvk], writes=[uvk])
                    S.op("dve", TT(uvb[:], uvb[:], pa, ALU.mult), reads=[uvk, "ps%d" % b], writes=[uvk])
                    S.op("act", ACTV(uvb[:], uvb[:], AF.Exp, scale=-1.5957691216057308), reads=[uvk], writes=[uvk])
                    S.op("dve", TS(uvb[:], uvb[:], 1.0, ALU.add), reads=[uvk], writes=[uvk])
                    S.op("dve", lambda e, uvb=uvb: e.reciprocal(out=uvb[:], in_=uvb[:]), reads=[uvk], writes=[uvk])
                    S.op("dve", TT(uvb[:], uvb[:], pa, ALU.mult), reads=[uvk, "ps%d" % b], writes=[uvk])
                ck(311)
                ck(31)
                vg = uvb[:, 256:512].rearrange("p (h d) -> p h d", h=4)
                ug = uvb[:, 0:256].rearrange("p (h d) -> p h d", h=4)
                S.op("dve", RED(st[:, 8:12], vg, ALU.add), reads=[uvk], writes=["stA"])
                S.op("dve", TT(sq[:], uvb[:, 256:512], uvb[:, 256:512], ALU.mult), reads=[uvk], writes=["sq"])
                S.op("dve", RED(st[:, 12:16], sq[:].rearrange("p (h d) -> p h d", h=4), ALU.add), reads=["sq"], writes=["stA"])
                S.op("dve", TS(st[:, 8:12], st[:, 8:12], 1.0 / 64, ALU.mult), reads=["stA"], writes=["stA"])
                S.op("dve", TT(st[:, 16:20], st[:, 8:12], st[:, 8:12], ALU.mult), reads=["stA"], writes=["stA"])
                S.op("dve", STT(st[:, 12:16], st[:, 12:16], 1.0 / 64, st[:, 16:20], ALU.mult, ALU.subtract),
                     reads=["stA"], writes=["stA"])
                S.op("act", ACTV(st[:, 12:16], st[:, 12:16], AF.Ln, bias=EPS, scale=1.0), reads=["stA"], writes=["stA"])
                S.op("act", ACTV(st[:, 12:16], st[:, 12:16], AF.Exp, scale=-0.5), reads=["stA"], writes=["stA"])
                S.op("dve", TT(sq[:].rearrange("p (h d) -> p h d", h=4), vg,
                               st[:, 8:12].unsqueeze(2).broadcast_to([128, 4, 64]), ALU.subtract),
                     reads=[uvk, "stA"], writes=["sq"])
                vnb = vn[n % 2]
                vnk = "vn%d" % (n % 2)
                S.op("dve", TT(vnb[:].rearrange("p (h d) -> p h d", h=4), sq[:].rearrange("p (h d) -> p h d", h=4),
                               st[:, 12:16].unsqueeze(2).broadcast_to([128, 4, 64]), ALU.mult),
                     reads=["sq", "stA"], writes=[vnk])
                b = nb()
                S.op("pe", seq([MM(ps[b][:, h * 64:(h + 1) * 64], WmT[:, h, :], vnb[:, h * 64:(h + 1) * 64]) for h in range(4)]),
                     reads=[vnk, "WmT"], writes=["ps%d" % b])
                yb = ytmp[0]
                S.op("dve", TT(yb[:].rearrange("p (h d) -> p h d", h=4), ps[b][:, 0:256].rearrange("p (h d) -> p h d", h=4),
                               sgub.unsqueeze(2).broadcast_to([128, 4, 64]), ALU.add),
                     reads=["ps%d" % b, "lp"], writes=["ytmp0"])
                S.op("dve", TT(yb[:], yb[:], uvb[:, 0:256], ALU.mult), reads=["ytmp0", uvk], writes=["ytmp0"])
                mixer_norm(yb, "ytmp0", ti, 0, col=1)

            chain_a = S.end_capture()
            S.begin_capture()
            bmode[0] = "mbc"
            for ti in range(4):
                n = 4 * g + ti
                uvb = uv[n % 2]
                uvk = "uv%d" % (n % 2)
                ck(32)
                b = nb()
                fs = []
                for wi in range(4):
                    o_ = ps[b][0:64, wi * 128:(wi + 1) * 128]
                    if n == 0:
                        fs.append(MM(o_, bin_[0][:, wi * 64:(wi + 1) * 64], band(wi, 2), start=True, stop=True))
                    else:
                        fs.append(MM(o_, bin_[n % 5][:, wi * 64:(wi + 1) * 64], band(wi, 0), start=True, stop=False))
                        fs.append(MM(o_, bin_[(n - 1) % 5][:, wi * 64:(wi + 1) * 64], band(wi, 1), start=False, stop=True))
                S.op("pe", seq(fs), reads=["bin%d" % (n % 5), "bin%d" % ((n - 1) % 5), "cbf"], writes=["ps%d" % b])
                inv = invc_first if n == 0 else wsc.unsqueeze(2).broadcast_to([64, 4, 128])
                S.op("dve", TT(yT[:], ps[b][0:64, :].rearrange("p (g t) -> p g t", g=4), inv, ALU.mult),
                     reads=["ps%d" % b, "c32"], writes=["yT"])
                b = nb()
                S.op("pe", seq([MM(ps[b][:, wi * 64:(wi + 1) * 64], yT[:, wi, :], wp[:, wi * 64:(wi + 1) * 64]) for wi in range(4)]),
                     reads=["yT", "wp"], writes=["ps%d" % b])
                yb = ytmp[1]
                S.op("dve", CP(yb[:], ps[b][:, 0:256]), reads=["ps%d" % b], writes=["ytmp1"])
                mixer_norm(yb, "ytmp1", ti, 1, col=2, junk=Pb[:, 0, :], junkkey="Pb")

                ck(33)
                if not (_os.environ.get("KVAR", "") == "5"):
                    nk = 1 if n == 0 else 2
                    k0 = n * 128 if n == 0 else (n - 1) * 128
                    bo = 128 if n == 0 else 0
                    W_ = nk * 128
                    banks = [nb(), nb()]
                    S.op("pe", seq([MM(ps[banks[p]][:, c * 256:c * 256 + W_],
                                       qTc[p * 64:(p + 1) * 64, c, ti * 128:(ti + 1) * 128],
                                       kTc[p * 64:(p + 1) * 64, k0:k0 + W_]) for c in range(2) for p in range(2)]),
                         reads=["qTc", "kTc"], writes=["ps%d" % banks[0], "ps%d" % banks[1]])
                    for p in range(2):
                        for c in range(2):
                            ho = p * 2 + c
                            S.op("dve", TT(swb[:, ho, 0:W_], ps[banks[p]][:, c * 256:c * 256 + W_], bias_tab[:, ho, bo:bo + W_], ALU.add),
                                 reads=["ps%d" % banks[p], "c32"], writes=["swb"])
                    ck(34)
                    S.op("dve", RED(st[:, 24:28], swb[:, :, 0:W_], ALU.max), reads=["swb"], writes=["stC"])
                    S.op("dve", TT(st[:, 24:28], st[:, 24:28], sinks, ALU.max), reads=["stC", "lp"], writes=["stC"])
                    S.op("dve", TT(st[:, 28:32], sinks, st[:, 24:28], ALU.subtract), reads=["stC", "lp"], writes=["stC"])
                    S.op("dve", TS(st[:, 24:28], st[:, 24:28], -1.0, ALU.mult), reads=["stC"], writes=["stC"])
                    S.op("act", ACTV(st[:, 28:32], st[:, 28:32], AF.Exp), reads=["stC"], writes=["stC"])
                    ck(35)
                    for ho in range(4):
                        S.op("act", ACTV(Pb[:, ho, 0:W_], swb[:, ho, 0:W_], AF.Exp, bias=st[:, 24 + ho:25 + ho], scale=1.0,
                                         accum_out=st[:, 32 + ho:33 + ho]),
                             reads=["swb", "stC"], writes=["Pb", "stC2"])
                    S.op("dve", TT(st[:, 32:36], st[:, 32:36], st[:, 28:32], ALU.add), reads=["stC", "stC2"], writes=["stC2"])
                    ck(36)
                    S.op("dve", lambda e: e.reciprocal(out=st[:, 32:36], in_=st[:, 32:36]), reads=["stC2"], writes=["stC2"])
                    ck(37)
                    b = nb()
                    S.op("pe", seq([TRN(psb[b][:, (ho * nk + kt) * 128:(ho * nk + kt + 1) * 128], Pb[:, ho, kt * 128:(kt + 1) * 128], ident)
                                    for ho in range(4) for kt in range(nk)]), reads=["Pb", "cbf"], writes=["ps%d" % b])
                    S.op("dve", CP(PT[:, 0:4 * nk, :], psb[b][:, 0:4 * nk * 128].rearrange("p (c t) -> p c t", c=4 * nk)),
                         reads=["ps%d" % b], writes=["PT"])
                    ck(38)
                    b = nb()
                    fs = []
                    for ho in range(4):
                        kv = ho // 2
                        for kt in range(nk):
                            fs.append(MM(ps[b][:, ho * 64:(ho + 1) * 64], PT[:, ho * nk + kt, :],
                                         vc[:, k0 // 128 + kt, kv * 64:(kv + 1) * 64], start=(kt == 0), stop=(kt == nk - 1)))
                    S.op("pe", seq(fs), reads=["PT", "vc"], writes=["ps%d" % b])
                    yb = ytmp[1]
                    S.op("dve", TT(yb[:].rearrange("p (h d) -> p h d", h=4), ps[b][:, 0:256].rearrange("p (h d) -> p h d", h=4),
                                   st[:, 32:36].unsqueeze(2).broadcast_to([128, 4, 64]), ALU.mult),
                         reads=["ps%d" % b, "stC2"], writes=["ytmp1"])
                    ck(390)
                    mixer_norm(yb, "ytmp1", ti, 2, col=3, junk=swb[:, 0, :], junkkey="swb")
                ck(40 + ti)

            chain_bc = S.end_capture()
            bmode[0] = "all"
            S.play([chain_sb, chain_a + chain_bc] if _os.environ.get("KNOSPLIT") else [chain_sb, chain_a, chain_bc])
            ck(5)

            ycTs = [(PT, "PT"), (Pb[:].rearrange("p h (c t) -> p (h c) t", t=128), "Pb")]

            def v_T(ti):
                buf, key = ycTs[ti % 2]
                b = nb()
                S.op("pe", seq([TRN(psb[b][:, j * 128:(j + 1) * 128], ycat[:, ti, j * 128:(j + 1) * 128], ident) for j in range(8)]),
                     reads=["ycat%d" % ti, "cbf"], writes=["ps%d" % b])
                S.op("dve", TT(buf[:, :, :], psb[b][:, :].rearrange("p (c t) -> p c t", c=8),
                               g2[:, 0:8].unsqueeze(2).broadcast_to([128, 8, 128]), ALU.mult),
                     reads=["ps%d" % b, "lp"], writes=[key])

            def v_M(ti):
                buf, key = ycTs[ti % 2]
                n = 4 * g + ti
                for half in range(2):
                    b = nb()
                    S.op("pe", seq([MM(ps[b][:, :], buf[:, kc, :], WS[4 + half][:, kc, :], start=(kc == 0), stop=(kc == 7))
                                    for kc in range(8)]), reads=[key, "ws%d" % (4 + half)], writes=["ps%d" % b])
                    S.op("dve", TT(x[:, n, half * 512:(half + 1) * 512], x[:, n, half * 512:(half + 1) * 512], ps[b][:, :], ALU.add),
                         reads=["ps%d" % b, "x%d" % n], writes=["x%d" % n])

            v_T(0)
            for ti in range(4):
                if ti + 1 < 4:
                    v_T(ti + 1)
                v_M(ti)

            ck(6)
        S.barrier()
        ck(7)
        def f1(gq):
            for n in range(4 * gq, 4 * gq + 4):
                if gq == 0:
                    jk, jkk = aT[0][:, 0:2, :].rearrange("p a b -> p (a b)"), "aT0"
                else:
                    jk, jkk = hn2[:], "hn2"
                norm_transpose(x[:, n, :], ["x%d" % n], hn2, "hn2", g3,
                               lambda n=n: h2T[:, :, n * 128:(n + 1) * 128],
                               ["h2T%d" % (n // 4)], col=0, junk=jk, junkkey=jkk)
        wgu_v = wgu_d[l].rearrange("(kc p) n -> p kc n", p=128)
        NCH = 6
        pend = None
        for c in range(NCH):
            nj = 4 if c < 5 else 2
            wcols = nj * 128
            if c == 5:
                sg_, su_, sd_, uo = 4, 4, 5, 256
            elif c % 2 == 0:
                sg_, su_, sd_, uo = 1, 2, 3, 0
            else:
                sg_, su_, sd_, uo = 4, 5, 0, 0
            load_w(sg_, WS[sg_][:, :, 0:wcols], wgu_v[:, :, c * 512:c * 512 + wcols])
            load_w(su_, WS[su_][:, :, uo:uo + wcols], wgu_v[:, :, DFF + c * 512:DFF + c * 512 + wcols])
            load_w(sd_, WSD[sd_][:, 0:nj, :], wdn_d[l, c * 512:c * 512 + wcols, :].rearrange("(j p) f -> p j f", p=128))
            for g in range(4):
                if c == 0:
                    f1(g)
                aTb = aT[g % 2]
                aTk = "aT%d" % (g % 2)
                for j in range(nj):
                    bg = nb()
                    S.op("pe", seq([MM(ps[bg][:, :], WS[sg_][:, kc, j * 128:(j + 1) * 128], h2T[:, kc, g * 512:(g + 1) * 512],
                                       start=(kc == 0), stop=(kc == 7)) for kc in range(8)]),
                         reads=["h2T%d" % g, "ws%d" % sg_], writes=["ps%d" % bg])
                    bu = nb()
                    S.op("pe", seq([MM(ps[bu][:, :], WS[su_][:, kc, uo + j * 128:uo + (j + 1) * 128], h2T[:, kc, g * 512:(g + 1) * 512],
                                       start=(kc == 0), stop=(kc == 7)) for kc in range(8)]),
                         reads=["h2T%d" % g, "ws%d" % su_], writes=["ps%d" % bu])
                    sgb = sg[j % 2]
                    S.op("act", ACTV(sgb[:], ps[bg][:, :], AF.Silu), reads=["ps%d" % bg], writes=["sg%d" % (j % 2)])
                    S.op("dve", TT(aTb[:, j, :], sgb[:], ps[bu][:, :], ALU.mult),
                         reads=["sg%d" % (j % 2), "ps%d" % bu], writes=[aTk])
                if pend is not None:
                    pend()

                def down(c=c, g=g, nj=nj, aTb=aTb, aTk=aTk, sd_=sd_):
                    for ti in range(4):
                        n = 4 * g + ti
                        for half in range(2):
                            b = nb()
                            S.op("pe", seq([MM(ps[b][:, :], aTb[:, j, ti * 128:(ti + 1) * 128], WSD[sd_][:, j, half * 512:(half + 1) * 512],
                                               start=(j == 0), stop=(j == nj - 1)) for j in range(nj)]),
                                 reads=[aTk, "ws%d" % sd_], writes=["ps%d" % b])
                            S.op("dve", TT(x[:, n, half * 512:(half + 1) * 512], x[:, n, half * 512:(half + 1) * 512], ps[b][:, :], ALU.add),
                                 reads=["ps%d" % b, "x%d" % n], writes=["x%d" % n])
                pend = down
        pend()
        S.barrier()

    def mixer_norm(yb, ykey, ti, m, col, junk=None, junkkey="sq"):
        if junk is None:
            junk = sq[:]
        S.op("act", ACTV(junk, yb[:], AF.Square, accum_out=st[:, col:col + 1]), reads=[ykey], writes=[junkkey, "st%d" % col])
        rstd_from_ss(col, 1.0 / 256)
        S.op("dve", TS(ycat[:, ti, m * 256:(m + 1) * 256], yb[:], st[:, col:col + 1], ALU.mult),
             reads=[ykey, "st%d" % col], writes=["ycat%d" % ti])

    def stick_breaking(g):
        amax = 4 * g + 3
        steps = [(h, a) for h in range(4) for a in range(amax, -1, -1)]
        nst = len(steps)
        info = {}

        def geom(i):
            h, a = steps[i]
            hc, hp = h // 2, h % 2
            pr = slice(hp * 64, (hp + 1) * 64)
            qlo = max(0, a - 4 * g)
            return h, a, hc, pr, qlo * 128, a >= 4 * g

        def pe_qk(i):
            h, a, hc, pr, c0, diag = geom(i)
            b1 = nb()
            info[i] = b1
            kslice = kTd[pr, hc, a * 128:(a + 1) * 128]
            fs = []
            if diag:
                fs.append(MM(ps[b1][:, c0:c0 + 128], kslice, qTd[pr, hc, c0:c0 + 128], start=True, stop=False, skip=True))
                if c0 + 128 < 512:
                    fs.append(MM(ps[b1][:, c0 + 128:512], kslice, qTd[pr, hc, c0 + 128:512], start=False, stop=False, skip=True))
                fs.append(MM(ps[b1][:, c0:c0 + 128], ident, maskb, start=False, stop=False, skip=True))
            else:
                fs.append(MM(ps[b1][:, 0:512], kslice, qTd[pr, hc, 0:512], start=True, stop=False, skip=True))
            S.op("pe", seq(fs), reads=["kTd", "qTd", "cbf"], writes=["ps%d" % b1])

        def act_e_lp(i):
            h, a, hc, pr, c0, diag = geom(i)
            b1 = info[i]
            S.op("act", ACTV(Eb[:, c0:512], ps[b1][:, c0:512], AF.Exp), reads=["ps%d" % b1], writes=["Eb"])
            lpb = Lp[i % 2]
            S.op("act", ACTV(lpb[:, c0:512], Eb[:, c0:512], AF.Ln, bias=1.0, scale=1.0), reads=["Eb"], writes=["Lp%d" % (i % 2)])
            rn = Rb[i % 2]
            ro = Rb[(i + 1) % 2]
            if a == amax:
                S.op("dve", seq([lambda e: e.memset(rn[:, 0:c0], 0.0), CP(rn[:, c0:512], lpb[:, c0:512])]),
                     reads=["Lp%d" % (i % 2)], writes=["R%d" % (i % 2)])
            elif a > 0:
                if c0 > 0:
                    S.op("dve", CP(rn[:, 0:c0], ro[:, 0:c0]), reads=["R%d" % ((i + 1) % 2)], writes=["R%d" % (i % 2)])
                S.op("dve", TT(rn[:, c0:512], ro[:, c0:512], lpb[:, c0:512], ALU.add),
                     reads=["R%d" % ((i + 1) % 2), "Lp%d" % (i % 2)], writes=["R%d" % (i % 2)])

        def pe_z2(i):
            h, a, hc, pr, c0, diag = geom(i)
            b1 = info[i]
            lpb = Lp[i % 2]
            ro = Rb[(i + 1) % 2]
            fs = []
            reads = ["cbf", "Lp%d" % (i % 2)]
            if diag:
                fs.append(MM(ps[b1][:, c0:c0 + 128], negtri, lpb[:, c0:c0 + 128], start=False, stop=False, skip=True))
                if c0 + 128 < 512:
                    fs.append(MM(ps[b1][:, c0 + 128:512], negtri, lpb[:, c0 + 128:512], start=False, stop=False, skip=True))
                    fs.append(MM(ps[b1][:, c0 + 128:512], negones, ro[:, c0 + 128:512], start=False, stop=True, skip=True))
                    reads.append("R%d" % ((i + 1) % 2))
            else:
                fs.append(MM(ps[b1][:, 0:512], negtri, lpb[:, 0:512], start=False, stop=False, skip=True))
                fs.append(MM(ps[b1][:, 0:512], negones, ro[:, 0:512], start=False, stop=True, skip=True))
                reads.append("R%d" % ((i + 1) % 2))
            keep_warm()
            S.op("pe", seq(fs), reads=reads, writes=["ps%d" % b1])

        def act_wt(i):
            h, a, hc, pr, c0, diag = geom(i)
            b1 = info[i]
            S.op("act", ACTV(wTb[i % 2][:, c0:512], ps[b1][:, c0:512], AF.Exp), reads=["ps%d" % b1], writes=["wT%d" % (i % 2)])

        def pe_pv(i):
            h, a, hc, pr, c0, diag = geom(i)
            pb = 6
            fs = []
            for ti in range(c0 // 128, 4):
                fs.append(MM(ps[pb][:, ti * 64:(ti + 1) * 64], wTb[i % 2][:, ti * 128:(ti + 1) * 128],
                             vd[:, a, h * 64:(h + 1) * 64], start=(a == amax), stop=(a == 0), skip=True))
            keep_warm()
            S.op("pe", seq(fs), reads=["wT%d" % (i % 2), "vd"], writes=["ps%d" % pb])
            if a == 0:
                S.op("dve", CP(ysb[:, :, h * 64:(h + 1) * 64], ps[pb][:, 0:256].rearrange("p (t d) -> p t d", t=4)),
                     reads=["ps%d" % pb], writes=["ysb"])

        pe_qk(0)
        for i in range(nst + 2):
            if 0 <= i - 1 < nst:
                pe_z2(i - 1)
            if 0 <= i - 2 < nst:
                pe_pv(i - 2)
            if i + 1 < nst:
                pe_qk(i + 1)
            if i < nst:
                act_e_lp(i)
            if 0 <= i - 1 < nst:
                act_wt(i - 1)
        for ti in range(4):
            S.op("act", ACTV(Eb[:, 0:256], ysb[:, ti, :], AF.Square, accum_out=st[:, 4:5]), reads=["ysb"], writes=["Eb", "st4"])
            rstd_from_ss(4, 1.0 / 256)
            S.op("dve", TS(ycat[:, ti, 768:1024], ysb[:, ti, :], st[:, 4:5], ALU.mult),
                 reads=["ysb", "st4"], writes=["ycat%d" % ti])


    for s in range(NS):
        for g in range(4):
            S.op("sp", DMAS([(x[:, 4 * g + i, :], x_d[s, (4 * g + i) * 128:(4 * g + i + 1) * 128, :]) for i in range(4)]),
                 writes=["x%d" % (4 * g + i) for i in range(4)], dma=("dx%d" % g, 4))
        try:
            for l in range(NL):
                layer(s, l, first=(s == 0 and l == 0), last_layer=(l == NL - 1))
        except _Cut:
            pass
        S.barrier(engs=("pe", "act", "dve", "sp"))
        S.op("sp", DMA(gFb[:], gf_d), writes=["gFb"], dma=("dgf", 1))
        for g in range(4):
            for i in range(4):
                n = 4 * g + i
                S.op("act", ACTV(hn2[:], x[:, n, :], AF.Square, accum_out=st[:, 0:1]), reads=["x%d" % n], writes=["hn2", "st0"])
                rstd_from_ss(0, 1.0 / D)
                S.op("dve", STT(x[:, n, :], x[:, n, :], st[:, 0:1], gFb[:], ALU.mult, ALU.mult),
                     reads=["x%d" % n, "st0", "gFb"], writes=["x%d" % n])
            S.op("sp", DMAS([(y_d[s, (4 * g + i) * 128:(4 * g + i + 1) * 128, :], x[:, 4 * g + i, :]) for i in range(4)]),
                 reads=["x%d" % (4 * g + i) for i in range(4)], writes=["y%d" % g], dma=("dy%d" % g, 4))
        S.barrier()
    S.op("sp", None, reads=["y%d" % g for g in range(4)])
    with ExitStack() as stack:
        S.emit(nc, stack)
    info = dict(sbuf_used=sbuf_used, nops=dict(S.nops), nsig=dict(S.n_signals))
    return nc, info


def _t5_bucket_np(dist):
    max_exact = 16
    df = np.maximum(dist, 1).astype(np.float32)
    large = max_exact + (np.log(df / np.float32(max_exact)) / np.float32(math.log(128 / max_exact))
                         * np.float32(32 - max_exact)).astype(np.int32)
    large = np.minimum(large, 31)
    return np.where(dist < max_exact, dist, large)


def _consts(rel_bias, norm_final):
    cb = np.zeros((128, NB_C), np.float32)
    i = np.arange(128)
    cb[:, 0:128] = np.eye(128)
    cb[:, 128:256] = -1.0 * (i[:, None] >= i[None, :])
    cb[:, 256:384] = -1.0
    cb[:, 384:512] = np.where(i[:, None] >= i[None, :], NEG, 0.0)
    cb[:, 512:640] = (i[:, None] <= i[None, :])
    for wi, w in enumerate(POOLW):
        s_ = i[:, None]
        t_ = i[None, :]
        inwin = ((t_ - s_) >= 0) & ((t_ - s_) < w)
        cur = inwin.astype(np.float32) - np.where(s_ == t_, float(w), 0.0)
        cnt = np.minimum(i + 1, w).astype(np.float32)
        cur_first = inwin.astype(np.float32) - np.where(s_ == t_, cnt[None, :], 0.0)
        prev = ((t_ + 128 - s_) < w).astype(np.float32)
        for kind, mat in enumerate((cur, prev, cur_first)):
            o = 640 + 128 * (wi * 3 + kind)
            cb[:, o:o + 128] = mat
    cbf = cb.astype(ml_dtypes.bfloat16)

    c32 = np.zeros((128, NF_C), np.float32)
    q = np.arange(128)[:, None]
    kc = np.arange(256)[None, :]
    dist = (q + 128) - kc
    inw = (dist >= 0) & (dist < 128)
    bucket = _t5_bucket_np(np.clip(dist, 0, 127))
    bg = rel_bias[bucket]
    c32[:, 0:1024] = np.transpose(bg, (0, 2, 1)).reshape(128, 1024)
    c32[:, 1024:1280] = np.where(inw, 0.0, NEG)
    for wi, w in enumerate(POOLW):
        c32[:, 1280 + wi * 128:1280 + (wi + 1) * 128] = (1.0 / np.minimum(np.arange(128) + 1, w))[None, :]
        c32[:, 1792 + wi] = 1.0 / w
    gf = np.ascontiguousarray(np.broadcast_to(norm_final[None, :], (128, D))).astype(np.float32)
    return cbf, c32, gf


def _prep_shared(w_in, w_out, sgu_w, sgu_b, pool_w, pool_scale, swa_sinks, rel_bias,
                 mix_out_gain, norm_mix, norm_ffn, w_gate_up, w_down, norm_final):
    f = lambda a: np.ascontiguousarray(np.asarray(a, dtype=np.float32))
    w_in, w_out, w_gate_up, w_down = f(w_in), f(w_out), f(w_gate_up), f(w_down)
    NL = w_in.shape[0]
    cq = [768 + h * 64 + d for h in (0, 2, 1, 3) for d in range(64)]
    perm = (list(range(0, 512)) + list(range(512, 768)) + list(range(1792, 2048)) + list(range(1152, 1280))
            + cq + list(range(1024, 1152)) + list(range(1280, 1536)) + list(range(1536, 1792)))
    w_in_p = np.ascontiguousarray(w_in[:, :, perm])
    lp = np.zeros((NL, 128, NP_LP), np.float32)
    lp[:, :, 0:8] = f(norm_mix).reshape(NL, 8, 128).transpose(0, 2, 1)
    lp[:, :, 8:16] = f(mix_out_gain).reshape(NL, 8, 128).transpose(0, 2, 1)
    lp[:, :, 16:24] = f(norm_ffn).reshape(NL, 8, 128).transpose(0, 2, 1)
    lp[:, :, 24:28] = f(sgu_b).transpose(0, 2, 1)
    lp[:, :, 28:32] = f(swa_sinks)[:, None, :]
    lp[:, :, 32:288] = f(pool_scale)[:, None, :]
    lp[:, :, 288:800] = f(sgu_w).transpose(0, 3, 1, 2).reshape(NL, 128, 512)
    lp[:, 0:64, 800:1056] = f(pool_w).transpose(0, 2, 1, 3).reshape(NL, 64, 256)
    cbf, c32, gf = _consts(f(rel_bias), f(norm_final))
    return dict(w_in=w_in_p, w_out=w_out, w_gu=w_gate_up, w_dn=w_down, lp=lp, cbf=cbf, c32=c32, gf=gf)


_CACHE = {}


def kernel(x, w_in, w_out, sgu_w, sgu_b, pool_w, pool_scale, swa_sinks, rel_bias,
           mix_out_gain, norm_mix, norm_ffn, w_gate_up, w_down, norm_final):
    x = np.asarray(x, dtype=np.float32)
    B = x.shape[0]
    NL = np.asarray(w_in).shape[0]
    n_cores = 8
    NS = B // n_cores
    shared = _prep_shared(w_in, w_out, sgu_w, sgu_b, pool_w, pool_scale, swa_sinks, rel_bias,
                          mix_out_gain, norm_mix, norm_ffn, w_gate_up, w_down, norm_final)
    key = (NL, NS)
    if key not in _CACHE:
        _CACHE[key] = build_program(NL, NS)
    nc, info = _CACHE[key]
    in_maps = []
    for c in range(n_cores):
        m = dict(shared)
        m["x"] = np.ascontiguousarray(x[c * NS:(c + 1) * NS])
        in_maps.append(m)
    res = run_bass_kernel_spmd(nc, in_maps, core_ids=list(range(n_cores)))
    out = np.concatenate([np.asarray(r["y"]) for r in res.results], axis=0)
    return out.astype(np.float32)
```

```python
import math
from contextlib import ExitStack

import numpy as np
import ml_dtypes
import concourse.bass as bass
import concourse.mybir as mybir
from concourse.bass_utils import run_bass_kernel_spmd

F32 = mybir.dt.float32
BF16 = mybir.dt.bfloat16
ALU = mybir.AluOpType
AF = mybir.ActivationFunctionType
AX = mybir.AxisListType

D = 1024
T = 2048
NT = 16
DFF = 2816
EPS = 1e-6
NEG = -30000.0
POOLW = (2, 4, 8, 16)
NP_LP = 1056
NB_C = 2176
NF_C = 1796
SEM_CH = 30000


class Sched:
    ENG = ("pe", "act", "dve", "pool", "sp")

    def __init__(self):
        self.ops = {e: [] for e in self.ENG}
        self.nops = {e: 0 for e in self.ENG}
        self.clock = {e: {} for e in self.ENG}
        self.lastw = {}
        self.readers = {}
        self.dma_cnt = {}
        self.waited_on = {e: set() for e in self.ENG}
        self.serialize = False
        self.last_sig = None
        self._cap = None

    def begin_capture(self):
        self._cap = []

    def end_capture(self):
        c = self._cap
        self._cap = None
        return c

    def play(self, lists):
        pos = [0] * len(lists)
        while True:
            best, bf = None, None
            for i, l in enumerate(lists):
                if pos[i] < len(l):
                    f = (pos[i] + 0.5) / len(l)
                    if bf is None or f < bf:
                        best, bf = i, f
            if best is None:
                break
            self.op(*lists[best][pos[best]])
            pos[best] += 1

    def op(self, eng, fn, reads=(), writes=(), dma=None):
        if self._cap is not None:
            self._cap.append((eng, fn, tuple(reads), tuple(writes), dma))
            return None
        raw, other = [], []
        if self.serialize and self.last_sig is not None:
            other.append(self.last_sig)
        for r in reads:
            s = self.lastw.get(r)
            if s is not None:
                raw.append(s)
            if r.startswith("ps"):
                other.extend(self.readers.get(r, ()))
        for w in writes:
            s = self.lastw.get(w)
            if s is not None:
                other.append(s)
            other.extend(self.readers.get(w, ()))
        clk = self.clock[eng]
        need = {}
        for is_raw, lst in ((True, raw), (False, other)):
            for (k, v, c) in lst:
                if clk.get(k, 0) >= v:
                    continue
                if k == eng:
                    if eng == "pe":
                        continue
                if k not in need or need[k][0] < v:
                    need[k] = (v, c)
        waits = []
        for k, (v, c) in need.items():
            if clk.get(k, 0) >= v:
                continue
            waits.append((k, v))
            for kk, vv in c.items():
                if clk.get(kk, 0) < vv:
                    clk[kk] = vv
            clk[k] = v
            if k in self.ENG:
                self.waited_on[k].add(v)
        if dma is None:
            self.nops[eng] += 1
            idx = self.nops[eng]
            sc = dict(clk)
            sc[eng] = idx
            sig = (eng, idx, sc)
            rec = dict(fn=fn, waits=waits, idx=idx, dma=None)
        else:
            dkey, dn = dma
            self.dma_cnt[dkey] = self.dma_cnt.get(dkey, 0) + dn
            v = self.dma_cnt[dkey]
            sc = dict(clk)
            sc[dkey] = v
            sig = (dkey, v, sc)
            rec = dict(fn=fn, waits=waits, idx=None, dma=dkey, dn=dn)
        self.ops[eng].append(rec)
        if fn is not None:
            self.last_sig = sig
        for r in reads:
            self.readers.setdefault(r, []).append(sig)
        for w in writes:
            self.lastw[w] = sig
            self.readers[w] = []
        return sig

    def barrier(self, engs=("pe", "act", "dve")):
        sigs = {}
        for e in engs:
            if self.nops[e] > 0:
                sc = dict(self.clock[e])
                sc[e] = self.nops[e]
                sigs[e] = (self.nops[e], sc)
        for e in engs:
            waits = []
            clk = self.clock[e]
            for k, (v, c) in sigs.items():
                if k == e or clk.get(k, 0) >= v:
                    continue
                waits.append((k, v))
                for kk, vv in c.items():
                    if clk.get(kk, 0) < vv:
                        clk[kk] = vv
                clk[k] = v
                self.waited_on[k].add(v)
            if waits:
                self.ops[e].append(dict(fn=None, waits=waits, idx=None, dma=None))

    def emit(self, nc, stack):
        valmap = {}
        nsem = {}
        for e in self.ENG:
            m = {}
            c = 0
            for i in range(1, self.nops[e] + 1):
                if i in self.waited_on[e]:
                    m[i] = (c // SEM_CH, c % SEM_CH + 1)
                    c += 1
            valmap[e] = m
            nsem[e] = c // SEM_CH + 1
        sems = {}
        for e in self.ENG:
            sems[e] = [stack.enter_context(nc.semaphore("s_%s%d" % (e, j))) for j in range(nsem[e])]
        for k in self.dma_cnt:
            sems[k] = stack.enter_context(nc.semaphore("d_" + str(k)))
        self.n_signals = {e: len(valmap[e]) for e in self.ENG}
        block = stack.enter_context(nc.Block())

        def mk(e):
            def run(engobj):
                for rec in self.ops[e]:
                    for (k, v) in rec["waits"]:
                        if k in self.ENG:
                            ep, val = valmap[k][v]
                            engobj.wait_ge(sems[k][ep], val)
                        else:
                            engobj.wait_ge(sems[k], 16 * v)
                    if rec["fn"] is None:
                        continue
                    ins = rec["fn"](engobj)
                    if rec["dma"] is not None:
                        assert len(ins) == rec["dn"]
                        for i_ in ins:
                            i_.then_inc(sems[rec["dma"]], 16)
                    elif rec["idx"] in valmap[e]:
                        ins.then_inc(sems[e][valmap[e][rec["idx"]][0]], 1)
            return run

        block.tensor(mk("pe"))
        block.scalar(mk("act"))
        block.vector(mk("dve"))
        block.gpsimd(mk("pool"))
        block.sync(mk("sp"))


def seq(fs):
    def run(e):
        ins = None
        for f in fs:
            ins = f(e)
        return ins
    return run


def MM(out, lhsT, rhs, start=True, stop=True, skip=False):
    if skip:
        return lambda e: e.matmul(out, lhsT=lhsT, rhs=rhs, start=start, stop=stop, skip_group_check=True)
    return lambda e: e.matmul(out, lhsT=lhsT, rhs=rhs, start=start, stop=stop)


def TRN(out, in_, ident):
    return lambda e: e.transpose(out, in_, ident)


def ACTV(out, in_, func, bias=None, scale=None, accum_out=None):
    kw = {}
    if bias is not None:
        kw["bias"] = bias
    if scale is not None:
        kw["scale"] = scale
    if accum_out is not None:
        kw["accum_out"] = accum_out
    return lambda e: e.activation(out=out, in_=in_, func=func, **kw)


def TT(out, in0, in1, op):
    return lambda e: e.tensor_tensor(out=out, in0=in0, in1=in1, op=op)


def TS(out, in0, s1, op0, s2=None, op1=None):
    if op1 is None:
        return lambda e: e.tensor_scalar(out=out, in0=in0, scalar1=s1, scalar2=None, op0=op0)
    return lambda e: e.tensor_scalar(out=out, in0=in0, scalar1=s1, scalar2=s2, op0=op0, op1=op1)


def STT(out, in0, scalar, in1, op0, op1):
    return lambda e: e.scalar_tensor_tensor(out=out, in0=in0, scalar=scalar, in1=in1, op0=op0, op1=op1)


def RED(out, in_, op):
    return lambda e: e.tensor_reduce(out=out, in_=in_, axis=AX.X, op=op)


def CP(out, in_):
    return lambda e: e.tensor_copy(out=out, in_=in_)


def DMA(out, in_):
    return lambda e: [e.dma_start(out=out, in_=in_)]


def DMAS(pairs):
    return lambda e: [e.dma_start(out=o, in_=i) for (o, i) in pairs]


class _Cut(Exception):
    pass


def build_program(NL=4, NS=2, dbg=None):
    nc = bass.Bass("TRN2", target_bir_lowering=False)

    ckcnt = {}

    def ck(k):
        if dbg is None:
            return
        kk, occ = dbg if isinstance(dbg, tuple) else (dbg, 1)
        if kk == k:
            ckcnt[k] = ckcnt.get(k, 0) + 1
            if ckcnt[k] == occ:
                raise _Cut()

    x_d = nc.dram_tensor("x", [NS, T, D], F32, kind="ExternalInput").ap()
    win_d = nc.dram_tensor("w_in", [NL, D, 2048], F32, kind="ExternalInput").ap()
    wout_d = nc.dram_tensor("w_out", [NL, D, D], F32, kind="ExternalInput").ap()
    wgu_d = nc.dram_tensor("w_gu", [NL, D, 2 * DFF], F32, kind="ExternalInput").ap()
    wdn_d = nc.dram_tensor("w_dn", [NL, DFF, D], F32, kind="ExternalInput").ap()
    lp_d = nc.dram_tensor("lp", [NL, 128, NP_LP], F32, kind="ExternalInput").ap()
    cbf_d = nc.dram_tensor("cbf", [128, NB_C], BF16, kind="ExternalInput").ap()
    c32_d = nc.dram_tensor("c32", [128, NF_C], F32, kind="ExternalInput").ap()
    gf_d = nc.dram_tensor("gf", [128, D], F32, kind="ExternalInput").ap()
    y_d = nc.dram_tensor("y", [NS, T, D], F32, kind="ExternalOutput").ap()

    S = Sched()
    import os as _os
    S.serialize = bool(_os.environ.get("KSERIAL"))
    base = [16512]
    LIMIT = 229376

    def sb(name, shape, dt, at=None):
        nb = int(np.prod(shape[1:])) * (4 if dt == F32 else 2)
        nb = (nb + 63) // 64 * 64
        if at is None:
            off = base[0]
            base[0] += nb
        else:
            off = at
        assert off + nb <= LIMIT, (name, off, nb)
        return nc.alloc_sbuf_tensor_at(name, list(shape), dt, offset=off), off + nb

    x, _ = sb("xres", [128, NT, D], F32)
    WS, WSD = [], []
    for i in range(6):
        off = base[0]
        h, _ = sb("ws%d" % i, [128, 8, 512], BF16)
        WS.append(h)
        h2, _ = sb("wsd%d" % i, [128, 4, 1024], BF16, at=off)
        WSD.append(h2)
    kTc, _ = sb("kTc", [128, T], BF16)
    kTd, _ = sb("kTd", [128, 2, T], BF16)
    vc, _ = sb("vc", [128, NT, 128], BF16)
    vd, _ = sb("vd", [128, NT, 256], BF16)
    cbf, _ = sb("cbf", [128, NB_C], BF16)
    c32, _ = sb("c32", [128, NF_C], F32)
    lp, _ = sb("lp", [128, NP_LP], F32)
    WmT, _ = sb("WmT", [128, 4, 128], BF16)
    wp, _ = sb("wp", [64, 256], BF16)
    st, _ = sb("stats", [128, 64], F32)
    zb, _ = sb("zb", [128, 512], BF16)
    ubase = base[0]

    hT, _ = sb("hT", [128, 8, 512], BF16)
    hn, _ = sb("hn", [128, D], BF16)
    qTc, _ = sb("qTc", [128, 2, 512], BF16)
    qTd, _ = sb("qTd", [128, 2, 512], BF16)
    uv = [sb("uv%d" % i, [128, 512], BF16)[0] for i in range(4)]
    bin_ = [sb("bin%d" % i, [128, 256], BF16)[0] for i in range(5)]
    vn = [sb("vn%d" % i, [128, 256], BF16)[0] for i in range(2)]
    sq, _ = sb("sq", [128, 256], F32)
    ytmp = [sb("ytmp%d" % i, [128, 256], F32)[0] for i in range(2)]
    ycat, _ = sb("ycat", [128, 4, D], BF16)
    yT, _ = sb("yT", [64, 4, 128], BF16)
    swb, _ = sb("swb", [128, 4, 256], F32)
    Pb, _ = sb("Pb", [128, 4, 256], BF16)
    PT, _ = sb("PT", [128, 8, 128], BF16)
    ycT = PT
    ysb, _ = sb("ysb", [128, 4, 256], F32)
    Eb, _ = sb("Eb", [128, 512], F32)
    Lp = [sb("Lp%d" % i, [128, 512], BF16)[0] for i in range(2)]
    wTb = [sb("wT%d" % i, [128, 512], BF16)[0] for i in range(2)]
    Rb = [sb("R%d" % i, [128, 512], BF16)[0] for i in range(2)]
    m_end = base[0]
    base[0] = ubase
    gFb, _ = sb("gFb", [128, D], F32, at=ubase)
    h2T, _ = sb("h2T", [128, 8, T], BF16)
    hn2, _ = sb("hn2", [128, D], BF16)
    sg = [sb("sg%d" % i, [128, 512], F32)[0] for i in range(2)]
    aT = [sb("aT%d" % i, [128, 4, 512], BF16)[0] for i in range(2)]
    f_end = base[0]
    sbuf_used = max(m_end, f_end)
    assert sbuf_used <= LIMIT

    ps = [nc.alloc_psum_tensor("ps%d" % i, [128, 512], F32) for i in range(8)]
    psb = [p.bitcast(BF16) for p in ps]
    POOLS = {"all": [0, 1, 2, 3, 4, 5, 7], "sb": [0, 1, 2], "ma": [3], "mbc": [5, 7]}
    rot = {"all": 0, "sb": 0, "ma": 0, "mbc": 0}
    bmode = ["all"]

    def nb():
        m = bmode[0]
        i = POOLS[m][rot[m]]
        rot[m] = (rot[m] + 1) % len(POOLS[m])
        return i

    ident = cbf[:, 0:128]
    negtri = cbf[:, 128:256]
    negones = cbf[:, 256:384]
    maskb = cbf[:, 384:512]
    mask01T = cbf[:, 512:640]

    def band(wi, kind):
        o = 640 + 128 * (wi * 3 + kind)
        return cbf[:, o:o + 128]

    bias_tab = c32[:, 0:1024].rearrange("p (h k) -> p h k", h=4)
    invc_first = c32[0:64, 1280:1792].rearrange("p (g t) -> p g t", g=4)
    wsc = c32[0:64, 1792:1796]
    g1 = lp[:, 0:8]
    g2 = lp[:, 8:16]
    g3 = lp[:, 16:24]
    sgub = lp[:, 24:28]
    sinks = lp[:, 28:32]

    S.op("sp", DMA(cbf[:], cbf_d), writes=["cbf"], dma=("dc0", 1))
    S.op("sp", DMA(c32[:], c32_d), writes=["c32"], dma=("dc1", 1))
    S.op("dve", TT(bias_tab, bias_tab, c32[:, 1024:1280].unsqueeze(1).broadcast_to([128, 4, 256]), ALU.add),
         reads=["c32"], writes=["c32"])

    S.op("dve", lambda e: e.memset(zb[:], 0.0), writes=["zb"])

    def keep_warm():
        S.op("pe", MM(ps[4][:, :], zb[:, 0:128], zb[:, :], start=True, stop=True), reads=["zb"], writes=["ps4"])

    def rstd_from_ss(col, n_inv):
        S.op("act", ACTV(st[:, col:col + 1], st[:, col:col + 1], AF.Ln, bias=EPS, scale=n_inv),
             reads=["st%d" % col], writes=["st%d" % col])
        S.op("act", ACTV(st[:, col:col + 1], st[:, col:col + 1], AF.Exp, scale=-0.5),
             reads=["st%d" % col], writes=["st%d" % col])

    def norm_transpose(src, srckeys, hnbuf, hnkey, gain, dst_full, dstkeys, col, junk, junkkey):
        S.op("act", ACTV(junk, src, AF.Square, accum_out=st[:, col:col + 1]),
             reads=srckeys, writes=[junkkey, "st%d" % col])
        rstd_from_ss(col, 1.0 / D)
        S.op("act", ACTV(hnbuf[:], src, AF.Copy, scale=st[:, col:col + 1]),
             reads=srckeys + ["st%d" % col], writes=[hnkey])
        b = nb()
        S.op("pe", seq([TRN(psb[b][:, j * 128:(j + 1) * 128], hnbuf[:, j * 128:(j + 1) * 128], ident) for j in range(8)]),
             reads=[hnkey, "cbf"], writes=["ps%d" % b])
        S.op("dve", TT(dst_full(), psb[b][:, :].rearrange("p (c t) -> p c t", c=8),
                       gain[:, 0:8].unsqueeze(2).broadcast_to([128, 8, 128]), ALU.mult),
             reads=["ps%d" % b, "lp"], writes=dstkeys)

    def load_w(slot, dst, src):
        S.op("pool", DMA(dst, src), writes=["ws%d" % slot], dma=("dw%d" % slot, 1))

    def layer(s, l, first, last_layer):
        S.op("sp", DMA(lp[:], lp_d[l]), writes=["lp"], dma=("dlp", 1))
        S.op("dve", TT(WmT[:], lp[:, 288:800].rearrange("p (h t) -> p h t", h=4),
                       mask01T.unsqueeze(1).broadcast_to([128, 4, 128]), ALU.mult),
             reads=["lp", "cbf"], writes=["WmT"])
        S.op("dve", TT(wp[:], lp[0:64, 800:1056], lp[0:64, 32:288], ALU.mult), reads=["lp"], writes=["wp"])
        win_v = win_d[l].rearrange("(kc p) n -> p kc n", p=128)
        wout_v = wout_d[l].rearrange("(kc p) n -> p kc n", p=128)
        for q in range(4):
            load_w(q, WS[q][:], win_v[:, :, q * 512:(q + 1) * 512])
        for q in range(2):
            load_w(4 + q, WS[4 + q][:], wout_v[:, :, q * 512:(q + 1) * 512])

        def win(kc, c0, c1):
            q = c0 // 512
            assert (c1 - 1) // 512 == q
            return WS[q][:, kc, c0 - q * 512:c1 - q * 512], "ws%d" % q

        ck(1)
        for g in range(4):
            for ti in range(4):
                n = 4 * g + ti
                norm_transpose(x[:, n, :], ["x%d" % n], hn, "hn", g1,
                               lambda ti=ti: hT[:, :, ti * 128:(ti + 1) * 128],
                               ["hT%d" % ti], col=0, junk=Pb[:].rearrange("p h k -> p (h k)"), junkkey="Pb")
            hTkeys = ["hT%d" % i for i in range(4)]
            ck(2)
            for fc in range(7):
                c0 = 1152 + fc * 128
                b = nb()
                fs = []
                wk = None
                for kc in range(8):
                    wap, wk = win(kc, c0, c0 + 128)
                    fs.append(MM(ps[b][:, :], wap, hT[:, kc, :], start=(kc == 0), stop=(kc == 7)))
                S.op("pe", seq(fs), reads=hTkeys + [wk], writes=["ps%d" % b])
                if fc < 2:
                    S.op("act", ACTV(qTc[:, fc, :], ps[b][:, :], AF.Copy, scale=0.125), reads=["ps%d" % b], writes=["qTc"])
                elif fc == 2:
                    S.op("act", ACTV(kTc[:, g * 512:(g + 1) * 512], ps[b][:, :], AF.Copy), reads=["ps%d" % b], writes=["kTc"])
                elif fc < 5:
                    S.op("act", ACTV(qTd[:, fc - 3, :], ps[b][:, :], AF.Copy, scale=0.125), reads=["ps%d" % b], writes=["qTd"])
                else:
                    S.op("dve", CP(kTd[:, fc - 5, g * 512:(g + 1) * 512], ps[b][:, :]), reads=["ps%d" % b], writes=["kTd"])
            ck(3)
            for ti in range(4):
                n = 4 * g + ti
                b = nb()
                S.op("pe", seq([MM(ps[b][:, :], hT[:, kc, ti * 128:(ti + 1) * 128], WS[0][:, kc, :], start=(kc == 0), stop=(kc == 7))
                                for kc in range(8)]), reads=["hT%d" % ti, "ws0"], writes=["ps%d" % b])
                S.op("act", ACTV(uv[ti][:], ps[b][:, :], AF.Gelu_apprx_tanh), reads=["ps%d" % b], writes=["uv%d" % ti])
            for ti in range(4):
                n = 4 * g + ti
                b = nb()
                S.op("pe", seq([MM(ps[b][:, :], hT[:, kc, ti * 128:(ti + 1) * 128], WS[1][:, kc, :], start=(kc == 0), stop=(kc == 7))
                                for kc in range(8)]), reads=["hT%d" % ti, "ws1"], writes=["ps%d" % b])
                S.op("dve", CP(bin_[n % 5][:], ps[b][:, 0:256]), reads=["ps%d" % b], writes=["bin%d" % (n % 5)])
                S.op("dve", CP(vd[:, n, :], ps[b][:, 256:512]), reads=["ps%d" % b], writes=["vd"])
                ck(312)
                b = nb()
                S.op("pe", seq([MM(ps[b][:, 0:128], hT[:, kc, ti * 128:(ti + 1) * 128], WS[2][:, kc, 0:128], start=(kc == 0), stop=(kc == 7))
                                for kc in range(8)]), reads=["hT%d" % ti, "ws2"], writes=["ps%d" % b])
                S.op("dve", CP(vc[:, n, :], ps[b][:, 0:128]), reads=["ps%d" % b], writes=["vc"])

            S.begin_capture()
            bmode[0] = "sb"
            stick_breaking(g)
            chain_sb = S.end_capture()
            S.begin_capture()
            bmode[0] = "ma"
            for ti in range(4):
                n = 4 * g + ti
                uvb = uv[ti]
                uvk = "uv%d" % ti
                ck(311)
                ck(31)
                vg = uvb[:, 256:512].rearrange("p (h d) -> p h d", h=4)
                ug = uvb[:, 0:256].rearrange("p (h d) -> p h d", h=4)
                S.op("dve", RED(st[:, 8:12], vg, ALU.add), reads=[uvk], writes=["stA"])
                S.op("dve", TT(sq[:], uvb[:, 256:512], uvb[:, 256:512], ALU.mult), reads=[uvk], writes=["sq"])
                S.op("dve", RED(st[:, 12:16], sq[:].rearrange("p (h d) -> p h d", h=4), ALU.add), reads=["sq"], writes=["stA"])
                S.op("dve", TS(st[:, 8:12], st[:, 8:12], 1.0 / 64, ALU.mult), reads=["stA"], writes=["stA"])
                S.op("dve", TT(st[:, 16:20], st[:, 8:12], st[:, 8:12], ALU.mult), reads=["stA"], writes=["stA"])
                S.op("dve", STT(st[:, 12:16], st[:, 12:16], 1.0 / 64, st[:, 16:20], ALU.mult, ALU.subtract),
                     reads=["stA"], writes=["stA"])
                S.op("act", ACTV(st[:, 12:16], st[:, 12:16], AF.Ln, bias=EPS, scale=1.0), reads=["stA"], writes=["stA"])
                S.op("act", ACTV(st[:, 12:16], st[:, 12:16], AF.Exp, scale=-0.5), reads=["stA"], writes=["stA"])
                S.op("dve", TT(sq[:].rearrange("p (h d) -> p h d", h=4), vg,
                               st[:, 8:12].unsqueeze(2).broadcast_to([128, 4, 64]), ALU.subtract),
                     reads=[uvk, "stA"], writes=["sq"])
                vnb = vn[n % 2]
                vnk = "vn%d" % (n % 2)
                S.op("dve", TT(vnb[:].rearrange("p (h d) -> p h d", h=4), sq[:].rearrange("p (h d) -> p h d", h=4),
                               st[:, 12:16].unsqueeze(2).broadcast_to([128, 4, 64]), ALU.mult),
                     reads=["sq", "stA"], writes=[vnk])
                b = nb()
                S.op("pe", seq([MM(ps[b][:, h * 64:(h + 1) * 64], WmT[:, h, :], vnb[:, h * 64:(h + 1) * 64]) for h in range(4)]),
                     reads=[vnk, "WmT"], writes=["ps%d" % b])
                yb = ytmp[0]
                S.op("dve", TT(yb[:].rearrange("p (h d) -> p h d", h=4), ps[b][:, 0:256].rearrange("p (h d) -> p h d", h=4),
                               sgub.unsqueeze(2).broadcast_to([128, 4, 64]), ALU.add),
                     reads=["ps%d" % b, "lp"], writes=["ytmp0"])
                S.op("dve", TT(yb[:], yb[:], uvb[:, 0:256], ALU.mult), reads=["ytmp0", uvk], writes=["ytmp0"])
                mixer_norm(yb, "ytmp0", ti, 0, col=1)

            chain_a = S.end_capture()
            S.begin_capture()
            bmode[0] = "mbc"
            for ti in range(4):
                n = 4 * g + ti
                uvb = uv[ti]
                uvk = "uv%d" % ti
                ck(32)
                b = nb()
                fs = []
                for wi in range(4):
                    o_ = ps[b][0:64, wi * 128:(wi + 1) * 128]
                    if n == 0:
                        fs.append(MM(o_, bin_[0][:, wi * 64:(wi + 1) * 64], band(wi, 2), start=True, stop=True))
                    else:
                        fs.append(MM(o_, bin_[n % 5][:, wi * 64:(wi + 1) * 64], band(wi, 0), start=True, stop=False))
                        fs.append(MM(o_, bin_[(n - 1) % 5][:, wi * 64:(wi + 1) * 64], band(wi, 1), start=False, stop=True))
                S.op("pe", seq(fs), reads=["bin%d" % (n % 5), "bin%d" % ((n - 1) % 5), "cbf"], writes=["ps%d" % b])
                inv = invc_first if n == 0 else wsc.unsqueeze(2).broadcast_to([64, 4, 128])
                S.op("dve", TT(yT[:], ps[b][0:64, :].rearrange("p (g t) -> p g t", g=4), inv, ALU.mult),
                     reads=["ps%d" % b, "c32"], writes=["yT"])
                b = nb()
                S.op("pe", seq([MM(ps[b][:, wi * 64:(wi + 1) * 64], yT[:, wi, :], wp[:, wi * 64:(wi + 1) * 64]) for wi in range(4)]),
                     reads=["yT", "wp"], writes=["ps%d" % b])
                yb = ytmp[1]
                S.op("dve", CP(yb[:], ps[b][:, 0:256]), reads=["ps%d" % b], writes=["ytmp1"])
                mixer_norm(yb, "ytmp1", ti, 1, col=2, junk=Pb[:, 0, :], junkkey="Pb")

                ck(33)
                if not (_os.environ.get("KVAR", "") == "5"):
                    nk = 1 if n == 0 else 2
                    k0 = n * 128 if n == 0 else (n - 1) * 128
                    bo = 128 if n == 0 else 0
                    W_ = nk * 128
                    banks = [nb(), nb()]
                    S.op("pe", seq([MM(ps[banks[p]][:, c * 256:c * 256 + W_],
                                       qTc[p * 64:(p + 1) * 64, c, ti * 128:(ti + 1) * 128],
                                       kTc[p * 64:(p + 1) * 64, k0:k0 + W_]) for c in range(2) for p in range(2)]),
                         reads=["qTc", "kTc"], writes=["ps%d" % banks[0], "ps%d" % banks[1]])
                    for p in range(2):
                        for c in range(2):
                            ho = p * 2 + c
                            S.op("dve", TT(swb[:, ho, 0:W_], ps[banks[p]][:, c * 256:c * 256 + W_], bias_tab[:, ho, bo:bo + W_], ALU.add),
                                 reads=["ps%d" % banks[p], "c32"], writes=["swb"])
                    ck(34)
                    S.op("dve", RED(st[:, 24:28], swb[:, :, 0:W_], ALU.max), reads=["swb"], writes=["stC"])
                    S.op("dve", TT(st[:, 24:28], st[:, 24:28], sinks, ALU.max), reads=["stC", "lp"], writes=["stC"])
                    S.op("dve", TT(st[:, 28:32], sinks, st[:, 24:28], ALU.subtract), reads=["stC", "lp"], writes=["stC"])
                    S.op("dve", TS(st[:, 24:28], st[:, 24:28], -1.0, ALU.mult), reads=["stC"], writes=["stC"])
                    S.op("act", ACTV(st[:, 28:32], st[:, 28:32], AF.Exp), reads=["stC"], writes=["stC"])
                    ck(35)
                    for ho in range(4):
                        S.op("act", ACTV(Pb[:, ho, 0:W_], swb[:, ho, 0:W_], AF.Exp, bias=st[:, 24 + ho:25 + ho], scale=1.0,
                                         accum_out=st[:, 32 + ho:33 + ho]),
                             reads=["swb", "stC"], writes=["Pb", "stC2"])
                    S.op("dve", TT(st[:, 32:36], st[:, 32:36], st[:, 28:32], ALU.add), reads=["stC", "stC2"], writes=["stC2"])
                    ck(36)
                    S.op("dve", lambda e: e.reciprocal(out=st[:, 32:36], in_=st[:, 32:36]), reads=["stC2"], writes=["stC2"])
                    ck(37)
                    b = nb()
                    S.op("pe", seq([TRN(psb[b][:, (ho * nk + kt) * 128:(ho * nk + kt + 1) * 128], Pb[:, ho, kt * 128:(kt + 1) * 128], ident)
                                    for ho in range(4) for kt in range(nk)]), reads=["Pb", "cbf"], writes=["ps%d" % b])
                    S.op("dve", CP(PT[:, 0:4 * nk, :], psb[b][:, 0:4 * nk * 128].rearrange("p (c t) -> p c t", c=4 * nk)),
                         reads=["ps%d" % b], writes=["PT"])
                    ck(38)
                    b = nb()
                    fs = []
                    for ho in range(4):
                        kv = ho // 2
                        for kt in range(nk):
                            fs.append(MM(ps[b][:, ho * 64:(ho + 1) * 64], PT[:, ho * nk + kt, :],
                                         vc[:, k0 // 128 + kt, kv * 64:(kv + 1) * 64], start=(kt == 0), stop=(kt == nk - 1)))
                    S.op("pe", seq(fs), reads=["PT", "vc"], writes=["ps%d" % b])
                    yb = ytmp[1]
                    S.op("dve", TT(yb[:].rearrange("p (h d) -> p h d", h=4), ps[b][:, 0:256].rearrange("p (h d) -> p h d", h=4),
                                   st[:, 32:36].unsqueeze(2).broadcast_to([128, 4, 64]), ALU.mult),
                         reads=["ps%d" % b, "stC2"], writes=["ytmp1"])
                    ck(390)
                    mixer_norm(yb, "ytmp1", ti, 2, col=3, junk=swb[:, 0, :], junkkey="swb")
                ck(40 + ti)

            chain_bc = S.end_capture()
            bmode[0] = "all"
            S.play([chain_sb, chain_a + chain_bc] if _os.environ.get("KNOSPLIT") else [chain_sb, chain_a, chain_bc])
            ck(5)

            ycTs = [(PT, "PT"), (Pb[:].rearrange("p h (c t) -> p (h c) t", t=128), "Pb")]

            def v_T(ti):
                buf, key = ycTs[ti % 2]
                b = nb()
                S.op("pe", seq([TRN(psb[b][:, j * 128:(j + 1) * 128], ycat[:, ti, j * 128:(j + 1) * 128], ident) for j in range(8)]),
                     reads=["ycat%d" % ti, "cbf"], writes=["ps%d" % b])
                S.op("dve", TT(buf[:, :, :], psb[b][:, :].rearrange("p (c t) -> p c t", c=8),
                               g2[:, 0:8].unsqueeze(2).broadcast_to([128, 8, 128]), ALU.mult),
                     reads=["ps%d" % b, "lp"], writes=[key])

            def v_M(ti):
                buf, key = ycTs[ti % 2]
                n = 4 * g + ti
                for half in range(2):
                    b = nb()
                    S.op("pe", seq([MM(ps[b][:, :], buf[:, kc, :], WS[4 + half][:, kc, :], start=(kc == 0), stop=(kc == 7))
                                    for kc in range(8)]), reads=[key, "ws%d" % (4 + half)], writes=["ps%d" % b])
                    S.op("dve", TT(x[:, n, half * 512:(half + 1) * 512], x[:, n, half * 512:(half + 1) * 512], ps[b][:, :], ALU.add),
                         reads=["ps%d" % b, "x%d" % n], writes=["x%d" % n])

            v_T(0)
            for ti in range(4):
                if ti + 1 < 4:
                    v_T(ti + 1)
                v_M(ti)

            ck(6)
        S.barrier()
        ck(7)
        def f1(gq):
            for n in range(4 * gq, 4 * gq + 4):
                if gq == 0:
                    jk, jkk = aT[0][:, 0:2, :].rearrange("p a b -> p (a b)"), "aT0"
                else:
                    jk, jkk = hn2[:], "hn2"
                norm_transpose(x[:, n, :], ["x%d" % n], hn2, "hn2", g3,
                               lambda n=n: h2T[:, :, n * 128:(n + 1) * 128],
                               ["h2T%d" % (n // 4)], col=0, junk=jk, junkkey=jkk)
        wgu_v = wgu_d[l].rearrange("(kc p) n -> p kc n", p=128)
        NCH = 6
        pend = None
        for c in range(NCH):
            nj = 4 if c < 5 else 2
            wcols = nj * 128
            par = c % 2
            sg_, su_, sd_ = 3 * par, 3 * par + 1, 3 * par + 2
            load_w(sg_, WS[sg_][:, :, 0:wcols], wgu_v[:, :, c * 512:c * 512 + wcols])
            load_w(su_, WS[su_][:, :, 0:wcols], wgu_v[:, :, DFF + c * 512:DFF + c * 512 + wcols])
            load_w(sd_, WSD[sd_][:, 0:nj, :], wdn_d[l, c * 512:c * 512 + wcols, :].rearrange("(j p) f -> p j f", p=128))
            for g in range(4):
                if c == 0:
                    f1(g)
                aTb = aT[g % 2]
                aTk = "aT%d" % (g % 2)
                for j in range(nj):
                    bg = nb()
                    S.op("pe", seq([MM(ps[bg][:, :], WS[sg_][:, kc, j * 128:(j + 1) * 128], h2T[:, kc, g * 512:(g + 1) * 512],
                                       start=(kc == 0), stop=(kc == 7)) for kc in range(8)]),
                         reads=["h2T%d" % g, "ws%d" % sg_], writes=["ps%d" % bg])
                    bu = nb()
                    S.op("pe", seq([MM(ps[bu][:, :], WS[su_][:, kc, j * 128:(j + 1) * 128], h2T[:, kc, g * 512:(g + 1) * 512],
                                       start=(kc == 0), stop=(kc == 7)) for kc in range(8)]),
                         reads=["h2T%d" % g, "ws%d" % su_], writes=["ps%d" % bu])
                    sgb = sg[j % 2]
                    S.op("act", ACTV(sgb[:], ps[bg][:, :], AF.Silu), reads=["ps%d" % bg], writes=["sg%d" % (j % 2)])
                    S.op("dve", TT(aTb[:, j, :], sgb[:], ps[bu][:, :], ALU.mult),
                         reads=["sg%d" % (j % 2), "ps%d" % bu], writes=[aTk])
                if pend is not None:
                    pend()

                def down(c=c, g=g, nj=nj, aTb=aTb, aTk=aTk, sd_=sd_):
                    for ti in range(4):
                        n = 4 * g + ti
                        for half in range(2):
                            b = nb()
                            S.op("pe", seq([MM(ps[b][:, :], aTb[:, j, ti * 128:(ti + 1) * 128], WSD[sd_][:, j, half * 512:(half + 1) * 512],
                                               start=(j == 0), stop=(j == nj - 1)) for j in range(nj)]),
                                 reads=[aTk, "ws%d" % sd_], writes=["ps%d" % b])
                            S.op("dve", TT(x[:, n, half * 512:(half + 1) * 512], x[:, n, half * 512:(half + 1) * 512], ps[b][:, :], ALU.add),
                                 reads=["ps%d" % b, "x%d" % n], writes=["x%d" % n])
                pend = down
        pend()
        S.barrier()

    def mixer_norm(yb, ykey, ti, m, col, junk=None, junkkey="sq"):
        if junk is None:
            junk = sq[:]
        S.op("act", ACTV(junk, yb[:], AF.Square, accum_out=st[:, col:col + 1]), reads=[ykey], writes=[junkkey, "st%d" % col])
        rstd_from_ss(col, 1.0 / 256)
        S.op("dve", TS(ycat[:, ti, m * 256:(m + 1) * 256], yb[:], st[:, col:col + 1], ALU.mult),
             reads=[ykey, "st%d" % col], writes=["ycat%d" % ti])

    def stick_breaking(g):
        amax = 4 * g + 3
        steps = [(h, a) for h in range(4) for a in range(amax, -1, -1)]
        nst = len(steps)
        info = {}

        def geom(i):
            h, a = steps[i]
            hc, hp = h // 2, h % 2
            pr = slice(hp * 64, (hp + 1) * 64)
            qlo = max(0, a - 4 * g)
            return h, a, hc, pr, qlo * 128, a >= 4 * g

        def pe_qk(i):
            h, a, hc, pr, c0, diag = geom(i)
            b1 = nb()
            info[i] = b1
            kslice = kTd[pr, hc, a * 128:(a + 1) * 128]
            fs = []
            if diag:
                fs.append(MM(ps[b1][:, c0:c0 + 128], kslice, qTd[pr, hc, c0:c0 + 128], start=True, stop=False, skip=True))
                if c0 + 128 < 512:
                    fs.append(MM(ps[b1][:, c0 + 128:512], kslice, qTd[pr, hc, c0 + 128:512], start=False, stop=False, skip=True))
                fs.append(MM(ps[b1][:, c0:c0 + 128], ident, maskb, start=False, stop=False, skip=True))
            else:
                fs.append(MM(ps[b1][:, 0:512], kslice, qTd[pr, hc, 0:512], start=True, stop=False, skip=True))
            S.op("pe", seq(fs), reads=["kTd", "qTd", "cbf"], writes=["ps%d" % b1])

        def act_e_lp(i):
            h, a, hc, pr, c0, diag = geom(i)
            b1 = info[i]
            S.op("act", ACTV(Eb[:, c0:512], ps[b1][:, c0:512], AF.Exp), reads=["ps%d" % b1], writes=["Eb"])
            lpb = Lp[i % 2]
            S.op("act", ACTV(lpb[:, c0:512], Eb[:, c0:512], AF.Ln, bias=1.0, scale=1.0), reads=["Eb"], writes=["Lp%d" % (i % 2)])
            rn = Rb[i % 2]
            ro = Rb[(i + 1) % 2]
            if a == amax:
                S.op("dve", seq([lambda e: e.memset(rn[:, 0:c0], 0.0), CP(rn[:, c0:512], lpb[:, c0:512])]),
                     reads=["Lp%d" % (i % 2)], writes=["R%d" % (i % 2)])
            elif a > 0:
                if c0 > 0:
                    S.op("dve", CP(rn[:, 0:c0], ro[:, 0:c0]), reads=["R%d" % ((i + 1) % 2)], writes=["R%d" % (i % 2)])
                S.op("dve", TT(rn[:, c0:512], ro[:, c0:512], lpb[:, c0:512], ALU.add),
                     reads=["R%d" % ((i + 1) % 2), "Lp%d" % (i % 2)], writes=["R%d" % (i % 2)])

        def pe_z2(i):
            h, a, hc, pr, c0, diag = geom(i)
            b1 = info[i]
            lpb = Lp[i % 2]
            ro = Rb[(i + 1) % 2]
            fs = []
            reads = ["cbf", "Lp%d" % (i % 2)]
            if diag:
                fs.append(MM(ps[b1][:, c0:c0 + 128], negtri, lpb[:, c0:c0 + 128], start=False, stop=False, skip=True))
                if c0 + 128 < 512:
                    fs.append(MM(ps[b1][:, c0 + 128:512], negtri, lpb[:, c0 + 128:512], start=False, stop=False, skip=True))
                    fs.append(MM(ps[b1][:, c0 + 128:512], negones, ro[:, c0 + 128:512], start=False, stop=True, skip=True))
                    reads.append("R%d" % ((i + 1) % 2))
            else:
                fs.append(MM(ps[b1][:, 0:512], negtri, lpb[:, 0:512], start=False, stop=False, skip=True))
                fs.append(MM(ps[b1][:, 0:512], negones, ro[:, 0:512], start=False, stop=True, skip=True))
                reads.append("R%d" % ((i + 1) % 2))
            keep_warm()
            S.op("pe", seq(fs), reads=reads, writes=["ps%d" % b1])

        def act_wt(i):
            h, a, hc, pr, c0, diag = geom(i)
            b1 = info[i]
            S.op("act", ACTV(wTb[i % 2][:, c0:512], ps[b1][:, c0:512], AF.Exp), reads=["ps%d" % b1], writes=["wT%d" % (i % 2)])

        def pe_pv(i):
            h, a, hc, pr, c0, diag = geom(i)
            pb = 6
            fs = []
            for ti in range(c0 // 128, 4):
                fs.append(MM(ps[pb][:, ti * 64:(ti + 1) * 64], wTb[i % 2][:, ti * 128:(ti + 1) * 128],
                             vd[:, a, h * 64:(h + 1) * 64], start=(a == amax), stop=(a == 0), skip=True))
            keep_warm()
            S.op("pe", seq(fs), reads=["wT%d" % (i % 2), "vd"], writes=["ps%d" % pb])
            if a == 0:
                S.op("dve", CP(ysb[:, :, h * 64:(h + 1) * 64], ps[pb][:, 0:256].rearrange("p (t d) -> p t d", t=4)),
                     reads=["ps%d" % pb], writes=["ysb"])

        pe_qk(0)
        for i in range(nst + 2):
            if 0 <= i - 1 < nst:
                pe_z2(i - 1)
            if 0 <= i - 2 < nst:
                pe_pv(i - 2)
            if i + 1 < nst:
                pe_qk(i + 1)
            if i < nst:
                act_e_lp(i)
            if 0 <= i - 1 < nst:
                act_wt(i - 1)
        for ti in range(4):
            S.op("act", ACTV(Eb[:, 0:256], ysb[:, ti, :], AF.Square, accum_out=st[:, 4:5]), reads=["ysb"], writes=["Eb", "st4"])
            rstd_from_ss(4, 1.0 / 256)
            S.op("dve", TS(ycat[:, ti, 768:1024], ysb[:, ti, :], st[:, 4:5], ALU.mult),
                 reads=["ysb", "st4"], writes=["ycat%d" % ti])


    for s in range(NS):
        for g in range(4):
            S.op("sp", DMAS([(x[:, 4 * g + i, :], x_d[s, (4 * g + i) * 128:(4 * g + i + 1) * 128, :]) for i in range(4)]),
                 writes=["x%d" % (4 * g + i) for i in range(4)], dma=("dx%d" % g, 4))
        try:
            for l in range(NL):
                layer(s, l, first=(s == 0 and l == 0), last_layer=(l == NL - 1))
        except _Cut:
            pass
        S.barrier(engs=("pe", "act", "dve", "sp"))
        S.op("sp", DMA(gFb[:], gf_d), writes=["gFb"], dma=("dgf", 1))
        for g in range(4):
            for i in range(4):
                n = 4 * g + i
                S.op("act", ACTV(hn2[:], x[:, n, :], AF.Square, accum_out=st[:, 0:1]), reads=["x%d" % n], writes=["hn2", "st0"])
                rstd_from_ss(0, 1.0 / D)
                S.op("dve", STT(x[:, n, :], x[:, n, :], st[:, 0:1], gFb[:], ALU.mult, ALU.mult),
                     reads=["x%d" % n, "st0", "gFb"], writes=["x%d" % n])
            S.op("sp", DMAS([(y_d[s, (4 * g + i) * 128:(4 * g + i + 1) * 128, :], x[:, 4 * g + i, :]) for i in range(4)]),
                 reads=["x%d" % (4 * g + i) for i in range(4)], writes=["y%d" % g], dma=("dy%d" % g, 4))
        S.barrier()
    S.op("sp", None, reads=["y%d" % g for g in range(4)])
    with ExitStack() as stack:
        S.emit(nc, stack)
    info = dict(sbuf_used=sbuf_used, nops=dict(S.nops), nsig=dict(S.n_signals))
    return nc, info


def _t5_bucket_np(dist):
    max_exact = 16
    df = np.maximum(dist, 1).astype(np.float32)
    large = max_exact + (np.log(df / np.float32(max_exact)) / np.float32(math.log(128 / max_exact))
                         * np.float32(32 - max_exact)).astype(np.int32)
    large = np.minimum(large, 31)
    return np.where(dist < max_exact, dist, large)


def _consts(rel_bias, norm_final):
    cb = np.zeros((128, NB_C), np.float32)
    i = np.arange(128)
    cb[:, 0:128] = np.eye(128)
    cb[:, 128:256] = -1.0 * (i[:, None] >= i[None, :])
    cb[:, 256:384] = -1.0
    cb[:, 384:512] = np.where(i[:, None] >= i[None, :], NEG, 0.0)
    cb[:, 512:640] = (i[:, None] <= i[None, :])
    for wi, w in enumerate(POOLW):
        s_ = i[:, None]
        t_ = i[None, :]
        inwin = ((t_ - s_) >= 0) & ((t_ - s_) < w)
        cur = inwin.astype(np.float32) - np.where(s_ == t_, float(w), 0.0)
        cnt = np.minimum(i + 1, w).astype(np.float32)
        cur_first = inwin.astype(np.float32) - np.where(s_ == t_, cnt[None, :], 0.0)
        prev = ((t_ + 128 - s_) < w).astype(np.float32)
        for kind, mat in enumerate((cur, prev, cur_first)):
            o = 640 + 128 * (wi * 3 + kind)
            cb[:, o:o + 128] = mat
    cbf = cb.astype(ml_dtypes.bfloat16)

    c32 = np.zeros((128, NF_C), np.float32)
    q = np.arange(128)[:, None]
    kc = np.arange(256)[None, :]
    dist = (q + 128) - kc
    inw = (dist >= 0) & (dist < 128)
    bucket = _t5_bucket_np(np.clip(dist, 0, 127))
    bg = rel_bias[bucket]
    c32[:, 0:1024] = np.transpose(bg, (0, 2, 1)).reshape(128, 1024)
    c32[:, 1024:1280] = np.where(inw, 0.0, NEG)
    for wi, w in enumerate(POOLW):
        c32[:, 1280 + wi * 128:1280 + (wi + 1) * 128] = (1.0 / np.minimum(np.arange(128) + 1, w))[None, :]
        c32[:, 1792 + wi] = 1.0 / w
    gf = np.ascontiguousarray(np.broadcast_to(norm_final[None, :], (128, D))).astype(np.float32)
    return cbf, c32, gf


def _prep_shared(w_in, w_out, sgu_w, sgu_b, pool_w, pool_scale, swa_sinks, rel_bias,
                 mix_out_gain, norm_mix, norm_ffn, w_gate_up, w_down, norm_final):
    f = lambda a: np.ascontiguousarray(np.asarray(a, dtype=np.float32))
    w_in, w_out, w_gate_up, w_down = f(w_in), f(w_out), f(w_gate_up), f(w_down)
    NL = w_in.shape[0]
    cq = [768 + h * 64 + d for h in (0, 2, 1, 3) for d in range(64)]
    perm = (list(range(0, 512)) + list(range(512, 768)) + list(range(1792, 2048)) + list(range(1152, 1280))
            + cq + list(range(1024, 1152)) + list(range(1280, 1536)) + list(range(1536, 1792)))
    w_in_p = np.ascontiguousarray(w_in[:, :, perm])
    lp = np.zeros((NL, 128, NP_LP), np.float32)
    lp[:, :, 0:8] = f(norm_mix).reshape(NL, 8, 128).transpose(0, 2, 1)
    lp[:, :, 8:16] = f(mix_out_gain).reshape(NL, 8, 128).transpose(0, 2, 1)
    lp[:, :, 16:24] = f(norm_ffn).reshape(NL, 8, 128).transpose(0, 2, 1)
    lp[:, :, 24:28] = f(sgu_b).transpose(0, 2, 1)
    lp[:, :, 28:32] = f(swa_sinks)[:, None, :]
    lp[:, :, 32:288] = f(pool_scale)[:, None, :]
    lp[:, :, 288:800] = f(sgu_w).transpose(0, 3, 1, 2).reshape(NL, 128, 512)
    lp[:, 0:64, 800:1056] = f(pool_w).transpose(0, 2, 1, 3).reshape(NL, 64, 256)
    cbf, c32, gf = _consts(f(rel_bias), f(norm_final))
    return dict(w_in=w_in_p, w_out=w_out, w_gu=w_gate_up, w_dn=w_down, lp=lp, cbf=cbf, c32=c32, gf=gf)


_CACHE = {}


def kernel(x, w_in, w_out, sgu_w, sgu_b, pool_w, pool_scale, swa_sinks, rel_bias,
           mix_out_gain, norm_mix, norm_ffn, w_gate_up, w_down, norm_final):
    x = np.asarray(x, dtype=np.float32)
    B = x.shape[0]
    NL = np.asarray(w_in).shape[0]
    n_cores = 8
    NS = B // n_cores
    shared = _prep_shared(w_in, w_out, sgu_w, sgu_b, pool_w, pool_scale, swa_sinks, rel_bias,
                          mix_out_gain, norm_mix, norm_ffn, w_gate_up, w_down, norm_final)
    key = (NL, NS)
    if key not in _CACHE:
        _CACHE[key] = build_program(NL, NS)
    nc, info = _CACHE[key]
    in_maps = []
    for c in range(n_cores):
        m = dict(shared)
        m["x"] = np.ascontiguousarray(x[c * NS:(c + 1) * NS])
        in_maps.append(m)
    res = run_bass_kernel_spmd(nc, in_maps, core_ids=list(range(n_cores)))
    out = np.concatenate([np.asarray(r["y"]) for r in res.results], axis=0)
    return out.astype(np.float32)
```

```python
import math
from contextlib import ExitStack

import numpy as np
import ml_dtypes
import concourse.bass as bass
import concourse.mybir as mybir
from concourse.bass_utils import run_bass_kernel_spmd

F32 = mybir.dt.float32
BF16 = mybir.dt.bfloat16
ALU = mybir.AluOpType
AF = mybir.ActivationFunctionType
AX = mybir.AxisListType

D = 1024
T = 2048
NT = 16
DFF = 2816
EPS = 1e-6
NEG = -30000.0
POOLW = (2, 4, 8, 16)
NP_LP = 1056
NB_C = 2176
NF_C = 1796
SEM_CH = 30000


class Sched:
    ENG = ("pe", "act", "dve", "pool", "sp")

    def __init__(self):
        self.ops = {e: [] for e in self.ENG}
        self.nops = {e: 0 for e in self.ENG}
        self.clock = {e: {} for e in self.ENG}
        self.lastw = {}
        self.readers = {}
        self.dma_cnt = {}
        self.waited_on = {e: set() for e in self.ENG}
        self.serialize = False
        self.last_sig = None
        self._cap = None

    def begin_capture(self):
        self._cap = []

    def end_capture(self):
        c = self._cap
        self._cap = None
        return c

    def play(self, lists):
        pos = [0] * len(lists)
        while True:
            best, bf = None, None
            for i, l in enumerate(lists):
                if pos[i] < len(l):
                    f = (pos[i] + 0.5) / len(l)
                    if bf is None or f < bf:
                        best, bf = i, f
            if best is None:
                break
            self.op(*lists[best][pos[best]])
            pos[best] += 1

    def op(self, eng, fn, reads=(), writes=(), dma=None):
        if self._cap is not None:
            self._cap.append((eng, fn, tuple(reads), tuple(writes), dma))
            return None
        raw, other = [], []
        if self.serialize and self.last_sig is not None:
            other.append(self.last_sig)
        for r in reads:
            s = self.lastw.get(r)
            if s is not None:
                raw.append(s)
            if r.startswith("ps"):
                other.extend(self.readers.get(r, ()))
        for w in writes:
            s = self.lastw.get(w)
            if s is not None:
                other.append(s)
            other.extend(self.readers.get(w, ()))
        clk = self.clock[eng]
        need = {}
        for is_raw, lst in ((True, raw), (False, other)):
            for (k, v, c) in lst:
                if clk.get(k, 0) >= v:
                    continue
                if k == eng:
                    if eng == "pe":
                        continue
                if k not in need or need[k][0] < v:
                    need[k] = (v, c)
        waits = []
        for k, (v, c) in need.items():
            if clk.get(k, 0) >= v:
                continue
            waits.append((k, v))
            for kk, vv in c.items():
                if clk.get(kk, 0) < vv:
                    clk[kk] = vv
            clk[k] = v
            if k in self.ENG:
                self.waited_on[k].add(v)
        if dma is None:
            self.nops[eng] += 1
            idx = self.nops[eng]
            sc = dict(clk)
            sc[eng] = idx
            sig = (eng, idx, sc)
            rec = dict(fn=fn, waits=waits, idx=idx, dma=None)
        else:
            dkey, dn = dma
            self.dma_cnt[dkey] = self.dma_cnt.get(dkey, 0) + dn
            v = self.dma_cnt[dkey]
            sc = dict(clk)
            sc[dkey] = v
            sig = (dkey, v, sc)
            rec = dict(fn=fn, waits=waits, idx=None, dma=dkey, dn=dn)
        self.ops[eng].append(rec)
        if fn is not None:
            self.last_sig = sig
        for r in reads:
            self.readers.setdefault(r, []).append(sig)
        for w in writes:
            self.lastw[w] = sig
            self.readers[w] = []
        return sig

    def barrier(self, engs=("pe", "act", "dve")):
        sigs = {}
        for e in engs:
            if self.nops[e] > 0:
                sc = dict(self.clock[e])
                sc[e] = self.nops[e]
                sigs[e] = (self.nops[e], sc)
        for e in engs:
            waits = []
            clk = self.clock[e]
            for k, (v, c) in sigs.items():
                if k == e or clk.get(k, 0) >= v:
                    continue
                waits.append((k, v))
                for kk, vv in c.items():
                    if clk.get(kk, 0) < vv:
                        clk[kk] = vv
                clk[k] = v
                self.waited_on[k].add(v)
            if waits:
                self.ops[e].append(dict(fn=None, waits=waits, idx=None, dma=None))

    def emit(self, nc, stack):
        valmap = {}
        nsem = {}
        for e in self.ENG:
            m = {}
            c = 0
            for i in range(1, self.nops[e] + 1):
                if i in self.waited_on[e]:
                    m[i] = (c // SEM_CH, c % SEM_CH + 1)
                    c += 1
            valmap[e] = m
            nsem[e] = c // SEM_CH + 1
        sems = {}
        for e in self.ENG:
            sems[e] = [stack.enter_context(nc.semaphore("s_%s%d" % (e, j))) for j in range(nsem[e])]
        for k in self.dma_cnt:
            sems[k] = stack.enter_context(nc.semaphore("d_" + str(k)))
        self.n_signals = {e: len(valmap[e]) for e in self.ENG}
        block = stack.enter_context(nc.Block())

        def mk(e):
            def run(engobj):
                for rec in self.ops[e]:
                    for (k, v) in rec["waits"]:
                        if k in self.ENG:
                            ep, val = valmap[k][v]
                            engobj.wait_ge(sems[k][ep], val)
                        else:
                            engobj.wait_ge(sems[k], 16 * v)
                    if rec["fn"] is None:
                        continue
                    ins = rec["fn"](engobj)
                    if rec["dma"] is not None:
                        assert len(ins) == rec["dn"]
                        for i_ in ins:
                            i_.then_inc(sems[rec["dma"]], 16)
                    elif rec["idx"] in valmap[e]:
                        ins.then_inc(sems[e][valmap[e][rec["idx"]][0]], 1)
            return run

        block.tensor(mk("pe"))
        block.scalar(mk("act"))
        block.vector(mk("dve"))
        block.gpsimd(mk("pool"))
        block.sync(mk("sp"))


def seq(fs):
    def run(e):
        ins = None
        for f in fs:
            ins = f(e)
        return ins
    return run


def MM(out, lhsT, rhs, start=True, stop=True, skip=False):
    if skip:
        return lambda e: e.matmul(out, lhsT=lhsT, rhs=rhs, start=start, stop=stop, skip_group_check=True)
    return lambda e: e.matmul(out, lhsT=lhsT, rhs=rhs, start=start, stop=stop)


def TRN(out, in_, ident):
    return lambda e: e.transpose(out, in_, ident)


def ACTV(out, in_, func, bias=None, scale=None, accum_out=None):
    kw = {}
    if bias is not None:
        kw["bias"] = bias
    if scale is not None:
        kw["scale"] = scale
    if accum_out is not None:
        kw["accum_out"] = accum_out
    return lambda e: e.activation(out=out, in_=in_, func=func, **kw)


def TT(out, in0, in1, op):
    return lambda e: e.tensor_tensor(out=out, in0=in0, in1=in1, op=op)


def TS(out, in0, s1, op0, s2=None, op1=None):
    if op1 is None:
        return lambda e: e.tensor_scalar(out=out, in0=in0, scalar1=s1, scalar2=None, op0=op0)
    return lambda e: e.tensor_scalar(out=out, in0=in0, scalar1=s1, scalar2=s2, op0=op0, op1=op1)


def STT(out, in0, scalar, in1, op0, op1):
    return lambda e: e.scalar_tensor_tensor(out=out, in0=in0, scalar=scalar, in1=in1, op0=op0, op1=op1)


def RED(out, in_, op):
    return lambda e: e.tensor_reduce(out=out, in_=in_, axis=AX.X, op=op)


def CP(out, in_):
    return lambda e: e.tensor_copy(out=out, in_=in_)


def DMA(out, in_):
    return lambda e: [e.dma_start(out=out, in_=in_)]


def DMAS(pairs):
    return lambda e: [e.dma_start(out=o, in_=i) for (o, i) in pairs]


class _Cut(Exception):
    pass


def build_program(NL=4, NS=2, dbg=None):
    nc = bass.Bass("TRN2", target_bir_lowering=False)

    ckcnt = {}

    def ck(k):
        if dbg is None:
            return
        kk, occ = dbg if isinstance(dbg, tuple) else (dbg, 1)
        if kk == k:
            ckcnt[k] = ckcnt.get(k, 0) + 1
            if ckcnt[k] == occ:
                raise _Cut()

    x_d = nc.dram_tensor("x", [NS, T, D], F32, kind="ExternalInput").ap()
    win_d = nc.dram_tensor("w_in", [NL, D, 2048], F32, kind="ExternalInput").ap()
    wout_d = nc.dram_tensor("w_out", [NL, D, D], F32, kind="ExternalInput").ap()
    wgu_d = nc.dram_tensor("w_gu", [NL, D, 2 * DFF], F32, kind="ExternalInput").ap()
    wdn_d = nc.dram_tensor("w_dn", [NL, DFF, D], F32, kind="ExternalInput").ap()
    lp_d = nc.dram_tensor("lp", [NL, 128, NP_LP], F32, kind="ExternalInput").ap()
    cbf_d = nc.dram_tensor("cbf", [128, NB_C], BF16, kind="ExternalInput").ap()
    c32_d = nc.dram_tensor("c32", [128, NF_C], F32, kind="ExternalInput").ap()
    gf_d = nc.dram_tensor("gf", [128, D], F32, kind="ExternalInput").ap()
    y_d = nc.dram_tensor("y", [NS, T, D], F32, kind="ExternalOutput").ap()

    S = Sched()
    import os as _os
    S.serialize = bool(_os.environ.get("KSERIAL"))
    base = [16512]
    LIMIT = 229376

    def sb(name, shape, dt, at=None):
        nb = int(np.prod(shape[1:])) * (4 if dt == F32 else 2)
        nb = (nb + 63) // 64 * 64
        if at is None:
            off = base[0]
            base[0] += nb
        else:
            off = at
        assert off + nb <= LIMIT, (name, off, nb)
        return nc.alloc_sbuf_tensor_at(name, list(shape), dt, offset=off), off + nb

    x, _ = sb("xres", [128, NT, D], F32)
    WS, WSD = [], []
    for i in range(6):
        off = base[0]
        h, _ = sb("ws%d" % i, [128, 8, 512], BF16)
        WS.append(h)
        h2, _ = sb("wsd%d" % i, [128, 4, 1024], BF16, at=off)
        WSD.append(h2)
    kTc, _ = sb("kTc", [128, T], BF16)
    kTd, _ = sb("kTd", [128, 2, T], BF16)
    vc, _ = sb("vc", [128, NT, 128], BF16)
    vd, _ = sb("vd", [128, NT, 256], BF16)
    cbf, _ = sb("cbf", [128, NB_C], BF16)
    c32, _ = sb("c32", [128, NF_C], F32)
    lp, _ = sb("lp", [128, NP_LP], F32)
    WmT, _ = sb("WmT", [128, 4, 128], BF16)
    wp, _ = sb("wp", [64, 256], BF16)
    st, _ = sb("stats", [128, 64], F32)
    zb, _ = sb("zb", [128, 512], BF16)
    ubase = base[0]

    hT, _ = sb("hT", [128, 8, 512], BF16)
    hn, _ = sb("hn", [128, D], BF16)
    qTc, _ = sb("qTc", [128, 2, 512], BF16)
    qTd, _ = sb("qTd", [128, 2, 512], BF16)
    uv = [sb("uv%d" % i, [128, 512], BF16)[0] for i in range(4)]
    bin_ = [sb("bin%d" % i, [128, 256], BF16)[0] for i in range(5)]
    vn = [sb("vn%d" % i, [128, 256], BF16)[0] for i in range(2)]
    sq, _ = sb("sq", [128, 256], F32)
    ytmp = [sb("ytmp%d" % i, [128, 256], F32)[0] for i in range(2)]
    ycat, _ = sb("ycat", [128, 4, D], BF16)
    yT, _ = sb("yT", [64, 4, 128], BF16)
    swb, _ = sb("swb", [128, 4, 256], F32)
    Pb, _ = sb("Pb", [128, 4, 256], BF16)
    PT, _ = sb("PT", [128, 8, 128], BF16)
    ycT = PT
    ysb, _ = sb("ysb", [128, 4, 256], F32)
    Eb, _ = sb("Eb", [128, 512], F32)
    Lp = [sb("Lp%d" % i, [128, 512], BF16)[0] for i in range(2)]
    wTb = [sb("wT%d" % i, [128, 512], BF16)[0] for i in range(2)]
    Rb = [sb("R%d" % i, [128, 512], BF16)[0] for i in range(2)]
    m_end = base[0]
    base[0] = ubase
    gFb, _ = sb("gFb", [128, D], F32, at=ubase)
    h2T, _ = sb("h2T", [128, 8, T], BF16)
    hn2, _ = sb("hn2", [128, D], BF16)
    sg = [sb("sg%d" % i, [128, 512], F32)[0] for i in range(2)]
    aT = [sb("aT%d" % i, [128, 4, 512], BF16)[0] for i in range(2)]
    f_end = base[0]
    sbuf_used = max(m_end, f_end)
    assert sbuf_used <= LIMIT

    ps = [nc.alloc_psum_tensor("ps%d" % i, [128, 512], F32) for i in range(8)]
    psb = [p.bitcast(BF16) for p in ps]
    POOLS = {"all": [0, 1, 2, 3, 4, 5, 7], "sb": [0, 1, 2], "ma": [3], "mbc": [5, 7]}
    rot = {"all": 0, "sb": 0, "ma": 0, "mbc": 0}
    bmode = ["all"]

    def nb():
        m = bmode[0]
        i = POOLS[m][rot[m]]
        rot[m] = (rot[m] + 1) % len(POOLS[m])
        return i

    ident = cbf[:, 0:128]
    negtri = cbf[:, 128:256]
    negones = cbf[:, 256:384]
    maskb = cbf[:, 384:512]
    mask01T = cbf[:, 512:640]

    def band(wi, kind):
        o = 640 + 128 * (wi * 3 + kind)
        return cbf[:, o:o + 128]

    bias_tab = c32[:, 0:1024].rearrange("p (h k) -> p h k", h=4)
    invc_first = c32[0:64, 1280:1792].rearrange("p (g t) -> p g t", g=4)
    wsc = c32[0:64, 1792:1796]
    g1 = lp[:, 0:8]
    g2 = lp[:, 8:16]
    g3 = lp[:, 16:24]
    sgub = lp[:, 24:28]
    sinks = lp[:, 28:32]

    S.op("sp", DMA(cbf[:], cbf_d), writes=["cbf"], dma=("dc0", 1))
    S.op("sp", DMA(c32[:], c32_d), writes=["c32"], dma=("dc1", 1))
    S.op("dve", TT(bias_tab, bias_tab, c32[:, 1024:1280].unsqueeze(1).broadcast_to([128, 4, 256]), ALU.add),
         reads=["c32"], writes=["c32"])

    S.op("dve", lambda e: e.memset(zb[:], 0.0), writes=["zb"])

    def keep_warm():
        S.op("pe", MM(ps[4][:, :], zb[:, 0:128], zb[:, :], start=True, stop=True), reads=["zb"], writes=["ps4"])

    def rstd_from_ss(col, n_inv):
        S.op("act", ACTV(st[:, col:col + 1], st[:, col:col + 1], AF.Ln, bias=EPS, scale=n_inv),
             reads=["st%d" % col], writes=["st%d" % col])
        S.op("act", ACTV(st[:, col:col + 1], st[:, col:col + 1], AF.Exp, scale=-0.5),
             reads=["st%d" % col], writes=["st%d" % col])

    def norm_transpose(src, srckeys, hnbuf, hnkey, gain, dst_full, dstkeys, col, junk, junkkey):
        S.op("act", ACTV(junk, src, AF.Square, accum_out=st[:, col:col + 1]),
             reads=srckeys, writes=[junkkey, "st%d" % col])
        rstd_from_ss(col, 1.0 / D)
        S.op("act", ACTV(hnbuf[:], src, AF.Copy, scale=st[:, col:col + 1]),
             reads=srckeys + ["st%d" % col], writes=[hnkey])
        b = nb()
        S.op("pe", seq([TRN(psb[b][:, j * 128:(j + 1) * 128], hnbuf[:, j * 128:(j + 1) * 128], ident) for j in range(8)]),
             reads=[hnkey, "cbf"], writes=["ps%d" % b])
        S.op("dve", TT(dst_full(), psb[b][:, :].rearrange("p (c t) -> p c t", c=8),
                       gain[:, 0:8].unsqueeze(2).broadcast_to([128, 8, 128]), ALU.mult),
             reads=["ps%d" % b, "lp"], writes=dstkeys)

    def load_w(slot, dst, src):
        S.op("pool", DMA(dst, src), writes=["ws%d" % slot], dma=("dw%d" % slot, 1))

    def layer(s, l, first, last_layer):
        S.op("sp", DMA(lp[:], lp_d[l]), writes=["lp"], dma=("dlp", 1))
        S.op("dve", TT(WmT[:], lp[:, 288:800].rearrange("p (h t) -> p h t", h=4),
                       mask01T.unsqueeze(1).broadcast_to([128, 4, 128]), ALU.mult),
             reads=["lp", "cbf"], writes=["WmT"])
        S.op("dve", TT(wp[:], lp[0:64, 800:1056], lp[0:64, 32:288], ALU.mult), reads=["lp"], writes=["wp"])
        win_v = win_d[l].rearrange("(kc p) n -> p kc n", p=128)
        wout_v = wout_d[l].rearrange("(kc p) n -> p kc n", p=128)
        for q in range(4):
            load_w(q, WS[q][:], win_v[:, :, q * 512:(q + 1) * 512])
        for q in range(2):
            load_w(4 + q, WS[4 + q][:], wout_v[:, :, q * 512:(q + 1) * 512])

        def win(kc, c0, c1):
            q = c0 // 512
            assert (c1 - 1) // 512 == q
            return WS[q][:, kc, c0 - q * 512:c1 - q * 512], "ws%d" % q

        ck(1)
        for g in range(4):
            for ti in range(4):
                n = 4 * g + ti
                norm_transpose(x[:, n, :], ["x%d" % n], hn, "hn", g1,
                               lambda ti=ti: hT[:, :, ti * 128:(ti + 1) * 128],
                               ["hT%d" % ti], col=0, junk=Pb[:].rearrange("p h k -> p (h k)"), junkkey="Pb")
            hTkeys = ["hT%d" % i for i in range(4)]
            ck(2)
            for fc in range(7):
                c0 = 1152 + fc * 128
                b = nb()
                fs = []
                wk = None
                for kc in range(8):
                    wap, wk = win(kc, c0, c0 + 128)
                    fs.append(MM(ps[b][:, :], wap, hT[:, kc, :], start=(kc == 0), stop=(kc == 7)))
                S.op("pe", seq(fs), reads=hTkeys + [wk], writes=["ps%d" % b])
                if fc < 2:
                    S.op("act", ACTV(qTc[:, fc, :], ps[b][:, :], AF.Copy, scale=0.125), reads=["ps%d" % b], writes=["qTc"])
                elif fc == 2:
                    S.op("act", ACTV(kTc[:, g * 512:(g + 1) * 512], ps[b][:, :], AF.Copy), reads=["ps%d" % b], writes=["kTc"])
                elif fc < 5:
                    S.op("act", ACTV(qTd[:, fc - 3, :], ps[b][:, :], AF.Copy, scale=0.125), reads=["ps%d" % b], writes=["qTd"])
                else:
                    S.op("dve", CP(kTd[:, fc - 5, g * 512:(g + 1) * 512], ps[b][:, :]), reads=["ps%d" % b], writes=["kTd"])
            ck(3)
            for ti in range(4):
                n = 4 * g + ti
                b = nb()
                S.op("pe", seq([MM(ps[b][:, :], hT[:, kc, ti * 128:(ti + 1) * 128], WS[0][:, kc, :], start=(kc == 0), stop=(kc == 7))
                                for kc in range(8)]), reads=["hT%d" % ti, "ws0"], writes=["ps%d" % b])
                S.op("act", ACTV(uv[ti][:], ps[b][:, :], AF.Gelu_apprx_tanh), reads=["ps%d" % b], writes=["uv%d" % ti])
            for ti in range(4):
                n = 4 * g + ti
                b = nb()
                S.op("pe", seq([MM(ps[b][:, :], hT[:, kc, ti * 128:(ti + 1) * 128], WS[1][:, kc, :], start=(kc == 0), stop=(kc == 7))
                                for kc in range(8)]), reads=["hT%d" % ti, "ws1"], writes=["ps%d" % b])
                S.op("dve", CP(bin_[n % 5][:], ps[b][:, 0:256]), reads=["ps%d" % b], writes=["bin%d" % (n % 5)])
                S.op("dve", CP(vd[:, n, :], ps[b][:, 256:512]), reads=["ps%d" % b], writes=["vd"])
                ck(312)
                b = nb()
                S.op("pe", seq([MM(ps[b][:, 0:128], hT[:, kc, ti * 128:(ti + 1) * 128], WS[2][:, kc, 0:128], start=(kc == 0), stop=(kc == 7))
                                for kc in range(8)]), reads=["hT%d" % ti, "ws2"], writes=["ps%d" % b])
                S.op("dve", CP(vc[:, n, :], ps[b][:, 0:128]), reads=["ps%d" % b], writes=["vc"])

            S.begin_capture()
            bmode[0] = "sb"
            stick_breaking(g)
            chain_sb = S.end_capture()
            S.begin_capture()
            bmode[0] = "ma"
            for ti in range(4):
                n = 4 * g + ti
                uvb = uv[ti]
                uvk = "uv%d" % ti
                ck(311)
                ck(31)
                vg = uvb[:, 256:512].rearrange("p (h d) -> p h d", h=4)
                ug = uvb[:, 0:256].rearrange("p (h d) -> p h d", h=4)
                S.op("dve", RED(st[:, 8:12], vg, ALU.add), reads=[uvk], writes=["stA"])
                S.op("dve", TT(sq[:], uvb[:, 256:512], uvb[:, 256:512], ALU.mult), reads=[uvk], writes=["sq"])
                S.op("dve", RED(st[:, 12:16], sq[:].rearrange("p (h d) -> p h d", h=4), ALU.add), reads=["sq"], writes=["stA"])
                S.op("dve", TS(st[:, 8:12], st[:, 8:12], 1.0 / 64, ALU.mult), reads=["stA"], writes=["stA"])
                S.op("dve", TT(st[:, 16:20], st[:, 8:12], st[:, 8:12], ALU.mult), reads=["stA"], writes=["stA"])
                S.op("dve", STT(st[:, 12:16], st[:, 12:16], 1.0 / 64, st[:, 16:20], ALU.mult, ALU.subtract),
                     reads=["stA"], writes=["stA"])
                S.op("act", ACTV(st[:, 12:16], st[:, 12:16], AF.Ln, bias=EPS, scale=1.0), reads=["stA"], writes=["stA"])
                S.op("act", ACTV(st[:, 12:16], st[:, 12:16], AF.Exp, scale=-0.5), reads=["stA"], writes=["stA"])
                S.op("dve", TT(sq[:].rearrange("p (h d) -> p h d", h=4), vg,
                               st[:, 8:12].unsqueeze(2).broadcast_to([128, 4, 64]), ALU.subtract),
                     reads=[uvk, "stA"], writes=["sq"])
                vnb = vn[n % 2]
                vnk = "vn%d" % (n % 2)
                S.op("dve", TT(vnb[:].rearrange("p (h d) -> p h d", h=4), sq[:].rearrange("p (h d) -> p h d", h=4),
                               st[:, 12:16].unsqueeze(2).broadcast_to([128, 4, 64]), ALU.mult),
                     reads=["sq", "stA"], writes=[vnk])
                b = nb()
                S.op("pe", seq([MM(ps[b][:, h * 64:(h + 1) * 64], WmT[:, h, :], vnb[:, h * 64:(h + 1) * 64]) for h in range(4)]),
                     reads=[vnk, "WmT"], writes=["ps%d" % b])
                yb = ytmp[0]
                S.op("dve", TT(yb[:].rearrange("p (h d) -> p h d", h=4), ps[b][:, 0:256].rearrange("p (h d) -> p h d", h=4),
                               sgub.unsqueeze(2).broadcast_to([128, 4, 64]), ALU.add),
                     reads=["ps%d" % b, "lp"], writes=["ytmp0"])
                S.op("dve", TT(yb[:], yb[:], uvb[:, 0:256], ALU.mult), reads=["ytmp0", uvk], writes=["ytmp0"])
                mixer_norm(yb, "ytmp0", ti, 0, col=1)

            chain_a = S.end_capture()
            S.begin_capture()
            bmode[0] = "mbc"
            for ti in range(4):
                n = 4 * g + ti
                uvb = uv[ti]
                uvk = "uv%d" % ti
                ck(32)
                b = nb()
                fs = []
                for wi in range(4):
                    o_ = ps[b][0:64, wi * 128:(wi + 1) * 128]
                    if n == 0:
                        fs.append(MM(o_, bin_[0][:, wi * 64:(wi + 1) * 64], band(wi, 2), start=True, stop=True))
                    else:
                        fs.append(MM(o_, bin_[n % 5][:, wi * 64:(wi + 1) * 64], band(wi, 0), start=True, stop=False))
                        fs.append(MM(o_, bin_[(n - 1) % 5][:, wi * 64:(wi + 1) * 64], band(wi, 1), start=False, stop=True))
                S.op("pe", seq(fs), reads=["bin%d" % (n % 5), "bin%d" % ((n - 1) % 5), "cbf"], writes=["ps%d" % b])
                inv = invc_first if n == 0 else wsc.unsqueeze(2).broadcast_to([64, 4, 128])
                S.op("dve", TT(yT[:], ps[b][0:64, :].rearrange("p (g t) -> p g t", g=4), inv, ALU.mult),
                     reads=["ps%d" % b, "c32"], writes=["yT"])
                b = nb()
                S.op("pe", seq([MM(ps[b][:, wi * 64:(wi + 1) * 64], yT[:, wi, :], wp[:, wi * 64:(wi + 1) * 64]) for wi in range(4)]),
                     reads=["yT", "wp"], writes=["ps%d" % b])
                yb = ytmp[1]
                S.op("dve", CP(yb[:], ps[b][:, 0:256]), reads=["ps%d" % b], writes=["ytmp1"])
                mixer_norm(yb, "ytmp1", ti, 1, col=2, junk=Pb[:, 0, :], junkkey="Pb")

                ck(33)
                if not (_os.environ.get("KVAR", "") == "5"):
                    nk = 1 if n == 0 else 2
                    k0 = n * 128 if n == 0 else (n - 1) * 128
                    bo = 128 if n == 0 else 0
                    W_ = nk * 128
                    banks = [nb(), nb()]
                    S.op("pe", seq([MM(ps[banks[p]][:, c * 256:c * 256 + W_],
                                       qTc[p * 64:(p + 1) * 64, c, ti * 128:(ti + 1) * 128],
                                       kTc[p * 64:(p + 1) * 64, k0:k0 + W_]) for c in range(2) for p in range(2)]),
                         reads=["qTc", "kTc"], writes=["ps%d" % banks[0], "ps%d" % banks[1]])
                    for p in range(2):
                        for c in range(2):
                            ho = p * 2 + c
                            S.op("dve", TT(swb[:, ho, 0:W_], ps[banks[p]][:, c * 256:c * 256 + W_], bias_tab[:, ho, bo:bo + W_], ALU.add),
                                 reads=["ps%d" % banks[p], "c32"], writes=["swb"])
                    ck(34)
                    S.op("dve", RED(st[:, 24:28], swb[:, :, 0:W_], ALU.max), reads=["swb"], writes=["stC"])
                    S.op("dve", TT(st[:, 24:28], st[:, 24:28], sinks, ALU.max), reads=["stC", "lp"], writes=["stC"])
                    S.op("dve", TT(st[:, 28:32], sinks, st[:, 24:28], ALU.subtract), reads=["stC", "lp"], writes=["stC"])
                    S.op("dve", TS(st[:, 24:28], st[:, 24:28], -1.0, ALU.mult), reads=["stC"], writes=["stC"])
                    S.op("act", ACTV(st[:, 28:32], st[:, 28:32], AF.Exp), reads=["stC"], writes=["stC"])
                    ck(35)
                    for ho in range(4):
                        S.op("act", ACTV(Pb[:, ho, 0:W_], swb[:, ho, 0:W_], AF.Exp, bias=st[:, 24 + ho:25 + ho], scale=1.0,
                                         accum_out=st[:, 32 + ho:33 + ho]),
                             reads=["swb", "stC"], writes=["Pb", "stC2"])
                    S.op("dve", TT(st[:, 32:36], st[:, 32:36], st[:, 28:32], ALU.add), reads=["stC", "stC2"], writes=["stC2"])
                    ck(36)
                    S.op("dve", lambda e: e.reciprocal(out=st[:, 32:36], in_=st[:, 32:36]), reads=["stC2"], writes=["stC2"])
                    ck(37)
                    b = nb()
                    S.op("pe", seq([TRN(psb[b][:, (ho * nk + kt) * 128:(ho * nk + kt + 1) * 128], Pb[:, ho, kt * 128:(kt + 1) * 128], ident)
                                    for ho in range(4) for kt in range(nk)]), reads=["Pb", "cbf"], writes=["ps%d" % b])
                    S.op("dve", CP(PT[:, 0:4 * nk, :], psb[b][:, 0:4 * nk * 128].rearrange("p (c t) -> p c t", c=4 * nk)),
                         reads=["ps%d" % b], writes=["PT"])
                    ck(38)
                    b = nb()
                    fs = []
                    for ho in range(4):
                        kv = ho // 2
                        for kt in range(nk):
                            fs.append(MM(ps[b][:, ho * 64:(ho + 1) * 64], PT[:, ho * nk + kt, :],
                                         vc[:, k0 // 128 + kt, kv * 64:(kv + 1) * 64], start=(kt == 0), stop=(kt == nk - 1)))
                    S.op("pe", seq(fs), reads=["PT", "vc"], writes=["ps%d" % b])
                    yb = ytmp[1]
                    S.op("dve", TT(yb[:].rearrange("p (h d) -> p h d", h=4), ps[b][:, 0:256].rearrange("p (h d) -> p h d", h=4),
                                   st[:, 32:36].unsqueeze(2).broadcast_to([128, 4, 64]), ALU.mult),
                         reads=["ps%d" % b, "stC2"], writes=["ytmp1"])
                    ck(390)
                    mixer_norm(yb, "ytmp1", ti, 2, col=3, junk=swb[:, 0, :], junkkey="swb")
                ck(40 + ti)

            chain_bc = S.end_capture()
            bmode[0] = "all"
            S.play([chain_sb, chain_a + chain_bc] if _os.environ.get("KNOSPLIT") else [chain_sb, chain_a, chain_bc])
            ck(5)

            ycTs = [(PT, "PT"), (Pb[:].rearrange("p h (c t) -> p (h c) t", t=128), "Pb")]

            def v_T(ti):
                buf, key = ycTs[ti % 2]
                b = nb()
                S.op("pe", seq([TRN(psb[b][:, j * 128:(j + 1) * 128], ycat[:, ti, j * 128:(j + 1) * 128], ident) for j in range(8)]),
                     reads=["ycat%d" % ti, "cbf"], writes=["ps%d" % b])
                S.op("dve", TT(buf[:, :, :], psb[b][:, :].rearrange("p (c t) -> p c t", c=8),
                               g2[:, 0:8].unsqueeze(2).broadcast_to([128, 8, 128]), ALU.mult),
                     reads=["ps%d" % b, "lp"], writes=[key])

            def v_M(ti):
                buf, key = ycTs[ti % 2]
                n = 4 * g + ti
                for half in range(2):
                    b = nb()
                    S.op("pe", seq([MM(ps[b][:, :], buf[:, kc, :], WS[4 + half][:, kc, :], start=(kc == 0), stop=(kc == 7))
                                    for kc in range(8)]), reads=[key, "ws%d" % (4 + half)], writes=["ps%d" % b])
                    S.op("dve", TT(x[:, n, half * 512:(half + 1) * 512], x[:, n, half * 512:(half + 1) * 512], ps[b][:, :], ALU.add),
                         reads=["ps%d" % b, "x%d" % n], writes=["x%d" % n])

            v_T(0)
            for ti in range(4):
                if ti + 1 < 4:
                    v_T(ti + 1)
                v_M(ti)

            ck(6)
        S.barrier()
        ck(7)
        def f1(gq):
            for n in range(4 * gq, 4 * gq + 4):
                if gq == 0:
                    jk, jkk = aT[0][:, 0:2, :].rearrange("p a b -> p (a b)"), "aT0"
                else:
                    jk, jkk = hn2[:], "hn2"
                norm_transpose(x[:, n, :], ["x%d" % n], hn2, "hn2", g3,
                               lambda n=n: h2T[:, :, n * 128:(n + 1) * 128],
                               ["h2T%d" % (n // 4)], col=0, junk=jk, junkkey=jkk)
        wgu_v = wgu_d[l].rearrange("(kc p) n -> p kc n", p=128)
        NCH = 6
        pend = None
        for c in range(NCH):
            nj = 4 if c < 5 else 2
            wcols = nj * 128
            par = c % 2
            sg_, su_, sd_ = 3 * par, 3 * par + 1, 3 * par + 2
            load_w(sg_, WS[sg_][:, :, 0:wcols], wgu_v[:, :, c * 512:c * 512 + wcols])
            load_w(su_, WS[su_][:, :, 0:wcols], wgu_v[:, :, DFF + c * 512:DFF + c * 512 + wcols])
            load_w(sd_, WSD[sd_][:, 0:nj, :], wdn_d[l, c * 512:c * 512 + wcols, :].rearrange("(j p) f -> p j f", p=128))
            for g in range(4):
                if c == 0:
                    f1(g)
                aTb = aT[g % 2]
                aTk = "aT%d" % (g % 2)
                for j in range(nj):
                    bg = nb()
                    S.op("pe", seq([MM(ps[bg][:, :], WS[sg_][:, kc, j * 128:(j + 1) * 128], h2T[:, kc, g * 512:(g + 1) * 512],
                                       start=(kc == 0), stop=(kc == 7)) for kc in range(8)]),
                         reads=["h2T%d" % g, "ws%d" % sg_], writes=["ps%d" % bg])
                    bu = nb()
                    S.op("pe", seq([MM(ps[bu][:, :], WS[su_][:, kc, j * 128:(j + 1) * 128], h2T[:, kc, g * 512:(g + 1) * 512],
                                       start=(kc == 0), stop=(kc == 7)) for kc in range(8)]),
                         reads=["h2T%d" % g, "ws%d" % su_], writes=["ps%d" % bu])
                    sgb = sg[j % 2]
                    S.op("act", ACTV(sgb[:], ps[bg][:, :], AF.Silu), reads=["ps%d" % bg], writes=["sg%d" % (j % 2)])
                    S.op("dve", TT(aTb[:, j, :], sgb[:], ps[bu][:, :], ALU.mult),
                         reads=["sg%d" % (j % 2), "ps%d" % bu], writes=[aTk])
                if pend is not None:
                    pend()

                def down(c=c, g=g, nj=nj, aTb=aTb, aTk=aTk, sd_=sd_):
                    for ti in range(4):
                        n = 4 * g + ti
                        for half in range(2):
                            b = nb()
                            S.op("pe", seq([MM(ps[b][:, :], aTb[:, j, ti * 128:(ti + 1) * 128], WSD[sd_][:, j, half * 512:(half + 1) * 512],
                                               start=(j == 0), stop=(j == nj - 1)) for j in range(nj)]),
                                 reads=[aTk, "ws%d" % sd_], writes=["ps%d" % b])
                            S.op("dve", TT(x[:, n, half * 512:(half + 1) * 512], x[:, n, half * 512:(half + 1) * 512], ps[b][:, :], ALU.add),
                                 reads=["ps%d" % b, "x%d" % n], writes=["x%d" % n])
                pend = down
        pend()
        S.barrier()

    def mixer_norm(yb, ykey, ti, m, col, junk=None, junkkey="sq"):
        if junk is None:
            junk = sq[:]
        S.op("act", ACTV(junk, yb[:], AF.Square, accum_out=st[:, col:col + 1]), reads=[ykey], writes=[junkkey, "st%d" % col])
        rstd_from_ss(col, 1.0 / 256)
        S.op("dve", TS(ycat[:, ti, m * 256:(m + 1) * 256], yb[:], st[:, col:col + 1], ALU.mult),
             reads=[ykey, "st%d" % col], writes=["ycat%d" % ti])

    def stick_breaking(g):
        amax = 4 * g + 3
        steps = [(h, a) for h in range(4) for a in range(amax, -1, -1)]
        nst = len(steps)
        info = {}

        def geom(i):
            h, a = steps[i]
            hc, hp = h // 2, h % 2
            pr = slice(hp * 64, (hp + 1) * 64)
            qlo = max(0, a - 4 * g)
            return h, a, hc, pr, qlo * 128, a >= 4 * g

        def pe_qk(i):
            h, a, hc, pr, c0, diag = geom(i)
            b1 = nb()
            info[i] = b1
            kslice = kTd[pr, hc, a * 128:(a + 1) * 128]
            fs = []
            if diag:
                fs.append(MM(ps[b1][:, c0:c0 + 128], kslice, qTd[pr, hc, c0:c0 + 128], start=True, stop=False, skip=True))
                if c0 + 128 < 512:
                    fs.append(MM(ps[b1][:, c0 + 128:512], kslice, qTd[pr, hc, c0 + 128:512], start=False, stop=False, skip=True))
                fs.append(MM(ps[b1][:, c0:c0 + 128], ident, maskb, start=False, stop=False, skip=True))
            else:
                fs.append(MM(ps[b1][:, 0:512], kslice, qTd[pr, hc, 0:512], start=True, stop=False, skip=True))
            S.op("pe", seq(fs), reads=["kTd", "qTd", "cbf"], writes=["ps%d" % b1])

        def act_e_lp(i):
            h, a, hc, pr, c0, diag = geom(i)
            b1 = info[i]
            S.op("act", ACTV(Eb[:, c0:512], ps[b1][:, c0:512], AF.Exp), reads=["ps%d" % b1], writes=["Eb"])
            lpb = Lp[i % 2]
            S.op("act", ACTV(lpb[:, c0:512], Eb[:, c0:512], AF.Ln, bias=1.0, scale=1.0), reads=["Eb"], writes=["Lp%d" % (i % 2)])
            rn = Rb[i % 2]
            ro = Rb[(i + 1) % 2]
            if a == amax:
                S.op("dve", seq([lambda e: e.memset(rn[:, 0:c0], 0.0), CP(rn[:, c0:512], lpb[:, c0:512])]),
                     reads=["Lp%d" % (i % 2)], writes=["R%d" % (i % 2)])
            elif a > 0:
                if c0 > 0:
                    S.op("dve", CP(rn[:, 0:c0], ro[:, 0:c0]), reads=["R%d" % ((i + 1) % 2)], writes=["R%d" % (i % 2)])
                S.op("dve", TT(rn[:, c0:512], ro[:, c0:512], lpb[:, c0:512], ALU.add),
                     reads=["R%d" % ((i + 1) % 2), "Lp%d" % (i % 2)], writes=["R%d" % (i % 2)])

        def pe_z2(i):
            h, a, hc, pr, c0, diag = geom(i)
            b1 = info[i]
            lpb = Lp[i % 2]
            ro = Rb[(i + 1) % 2]
            fs = []
            reads = ["cbf", "Lp%d" % (i % 2)]
            if diag:
                fs.append(MM(ps[b1][:, c0:c0 + 128], negtri, lpb[:, c0:c0 + 128], start=False, stop=False, skip=True))
                if c0 + 128 < 512:
                    fs.append(MM(ps[b1][:, c0 + 128:512], negtri, lpb[:, c0 + 128:512], start=False, stop=False, skip=True))
                    fs.append(MM(ps[b1][:, c0 + 128:512], negones, ro[:, c0 + 128:512], start=False, stop=True, skip=True))
                    reads.append("R%d" % ((i + 1) % 2))
            else:
                fs.append(MM(ps[b1][:, 0:512], negtri, lpb[:, 0:512], start=False, stop=False, skip=True))
                fs.append(MM(ps[b1][:, 0:512], negones, ro[:, 0:512], start=False, stop=True, skip=True))
                reads.append("R%d" % ((i + 1) % 2))
            keep_warm()
            S.op("pe", seq(fs), reads=reads, writes=["ps%d" % b1])

        def act_wt(i):
            h, a, hc, pr, c0, diag = geom(i)
            b1 = info[i]
            S.op("act", ACTV(wTb[i % 2][:, c0:512], ps[b1][:, c0:512], AF.Exp), reads=["ps%d" % b1], writes=["wT%d" % (i % 2)])

        def pe_pv(i):
            h, a, hc, pr, c0, diag = geom(i)
            pb = 6
            fs = []
            for ti in range(c0 // 128, 4):
                fs.append(MM(ps[pb][:, ti * 64:(ti + 1) * 64], wTb[i % 2][:, ti * 128:(ti + 1) * 128],
                             vd[:, a, h * 64:(h + 1) * 64], start=(a == amax), stop=(a == 0), skip=True))
            S.op("pe", seq(fs), reads=["wT%d" % (i % 2), "vd"], writes=["ps%d" % pb])
            if a == 0:
                S.op("dve", CP(ysb[:, :, h * 64:(h + 1) * 64], ps[pb][:, 0:256].rearrange("p (t d) -> p t d", t=4)),
                     reads=["ps%d" % pb], writes=["ysb"])

        pe_qk(0)
        for i in range(nst + 2):
            if 0 <= i - 1 < nst:
                pe_z2(i - 1)
            if 0 <= i - 2 < nst:
                pe_pv(i - 2)
            if i + 1 < nst:
                pe_qk(i + 1)
            if i < nst:
                act_e_lp(i)
            if 0 <= i - 1 < nst:
                act_wt(i - 1)
        for ti in range(4):
            S.op("act", ACTV(Eb[:, 0:256], ysb[:, ti, :], AF.Square, accum_out=st[:, 4:5]), reads=["ysb"], writes=["Eb", "st4"])
            rstd_from_ss(4, 1.0 / 256)
            S.op("dve", TS(ycat[:, ti, 768:1024], ysb[:, ti, :], st[:, 4:5], ALU.mult),
                 reads=["ysb", "st4"], writes=["ycat%d" % ti])


    for s in range(NS):
        for g in range(4):
            S.op("sp", DMAS([(x[:, 4 * g + i, :], x_d[s, (4 * g + i) * 128:(4 * g + i + 1) * 128, :]) for i in range(4)]),
                 writes=["x%d" % (4 * g + i) for i in range(4)], dma=("dx%d" % g, 4))
        try:
            for l in range(NL):
                layer(s, l, first=(s == 0 and l == 0), last_layer=(l == NL - 1))
        except _Cut:
            pass
        S.barrier(engs=("pe", "act", "dve", "sp"))
        S.op("sp", DMA(gFb[:], gf_d), writes=["gFb"], dma=("dgf", 1))
        for g in range(4):
            for i in range(4):
                n = 4 * g + i
                S.op("act", ACTV(hn2[:], x[:, n, :], AF.Square, accum_out=st[:, 0:1]), reads=["x%d" % n], writes=["hn2", "st0"])
                rstd_from_ss(0, 1.0 / D)
                S.op("dve", STT(x[:, n, :], x[:, n, :], st[:, 0:1], gFb[:], ALU.mult, ALU.mult),
                     reads=["x%d" % n, "st0", "gFb"], writes=["x%d" % n])
            S.op("sp", DMAS([(y_d[s, (4 * g + i) * 128:(4 * g + i + 1) * 128, :], x[:, 4 * g + i, :]) for i in range(4)]),
                 reads=["x%d" % (4 * g + i) for i in range(4)], writes=["y%d" % g], dma=("dy%d" % g, 4))
        S.barrier()
    S.op("sp", None, reads=["y%d" % g for g in range(4)])
    with ExitStack() as stack:
        S.emit(nc, stack)
    info = dict(sbuf_used=sbuf_used, nops=dict(S.nops), nsig=dict(S.n_signals))
    return nc, info


def _t5_bucket_np(dist):
    max_exact = 16
    df = np.maximum(dist, 1).astype(np.float32)
    large = max_exact + (np.log(df / np.float32(max_exact)) / np.float32(math.log(128 / max_exact))
                         * np.float32(32 - max_exact)).astype(np.int32)
    large = np.minimum(large, 31)
    return np.where(dist < max_exact, dist, large)


def _consts(rel_bias, norm_final):
    cb = np.zeros((128, NB_C), np.float32)
    i = np.arange(128)
    cb[:, 0:128] = np.eye(128)
    cb[:, 128:256] = -1.0 * (i[:, None] >= i[None, :])
    cb[:, 256:384] = -1.0
    cb[:, 384:512] = np.where(i[:, None] >= i[None, :], NEG, 0.0)
    cb[:, 512:640] = (i[:, None] <= i[None, :])
    for wi, w in enumerate(POOLW):
        s_ = i[:, None]
        t_ = i[None, :]
        inwin = ((t_ - s_) >= 0) & ((t_ - s_) < w)
        cur = inwin.astype(np.float32) - np.where(s_ == t_, float(w), 0.0)
        cnt = np.minimum(i + 1, w).astype(np.float32)
        cur_first = inwin.astype(np.float32) - np.where(s_ == t_, cnt[None, :], 0.0)
        prev = ((t_ + 128 - s_) < w).astype(np.float32)
        for kind, mat in enumerate((cur, prev, cur_first)):
            o = 640 + 128 * (wi * 3 + kind)
            cb[:, o:o + 128] = mat
    cbf = cb.astype(ml_dtypes.bfloat16)

    c32 = np.zeros((128, NF_C), np.float32)
    q = np.arange(128)[:, None]
    kc = np.arange(256)[None, :]
    dist = (q + 128) - kc
    inw = (dist >= 0) & (dist < 128)
    bucket = _t5_bucket_np(np.clip(dist, 0, 127))
    bg = rel_bias[bucket]
    c32[:, 0:1024] = np.transpose(bg, (0, 2, 1)).reshape(128, 1024)
    c32[:, 1024:1280] = np.where(inw, 0.0, NEG)
    for wi, w in enumerate(POOLW):
        c32[:, 1280 + wi * 128:1280 + (wi + 1) * 128] = (1.0 / np.minimum(np.arange(128) + 1, w))[None, :]
        c32[:, 1792 + wi] = 1.0 / w
    gf = np.ascontiguousarray(np.broadcast_to(norm_final[None, :], (128, D))).astype(np.float32)
    return cbf, c32, gf


def _prep_shared(w_in, w_out, sgu_w, sgu_b, pool_w, pool_scale, swa_sinks, rel_bias,
                 mix_out_gain, norm_mix, norm_ffn, w_gate_up, w_down, norm_final):
    f = lambda a: np.ascontiguousarray(np.asarray(a, dtype=np.float32))
    w_in, w_out, w_gate_up, w_down = f(w_in), f(w_out), f(w_gate_up), f(w_down)
    NL = w_in.shape[0]
    cq = [768 + h * 64 + d for h in (0, 2, 1, 3) for d in range(64)]
    perm = (list(range(0, 512)) + list(range(512, 768)) + list(range(1792, 2048)) + list(range(1152, 1280))
            + cq + list(range(1024, 1152)) + list(range(1280, 1536)) + list(range(1536, 1792)))
    w_in_p = np.ascontiguousarray(w_in[:, :, perm])
    lp = np.zeros((NL, 128, NP_LP), np.float32)
    lp[:, :, 0:8] = f(norm_mix).reshape(NL, 8, 128).transpose(0, 2, 1)
    lp[:, :, 8:16] = f(mix_out_gain).reshape(NL, 8, 128).transpose(0, 2, 1)
    lp[:, :, 16:24] = f(norm_ffn).reshape(NL, 8, 128).transpose(0, 2, 1)
    lp[:, :, 24:28] = f(sgu_b).transpose(0, 2, 1)
    lp[:, :, 28:32] = f(swa_sinks)[:, None, :]
    lp[:, :, 32:288] = f(pool_scale)[:, None, :]
    lp[:, :, 288:800] = f(sgu_w).transpose(0, 3, 1, 2).reshape(NL, 128, 512)
    lp[:, 0:64, 800:1056] = f(pool_w).transpose(0, 2, 1, 3).reshape(NL, 64, 256)
    cbf, c32, gf = _consts(f(rel_bias), f(norm_final))
    return dict(w_in=w_in_p, w_out=w_out, w_gu=w_gate_up, w_dn=w_down, lp=lp, cbf=cbf, c32=c32, gf=gf)


_CACHE = {}


def kernel(x, w_in, w_out, sgu_w, sgu_b, pool_w, pool_scale, swa_sinks, rel_bias,
           mix_out_gain, norm_mix, norm_ffn, w_gate_up, w_down, norm_final):
    x = np.asarray(x, dtype=np.float32)
    B = x.shape[0]
    NL = np.asarray(w_in).shape[0]
    n_cores = 8
    NS = B // n_cores
    shared = _prep_shared(w_in, w_out, sgu_w, sgu_b, pool_w, pool_scale, swa_sinks, rel_bias,
                          mix_out_gain, norm_mix, norm_ffn, w_gate_up, w_down, norm_final)
    key = (NL, NS)
    if key not in _CACHE:
        _CACHE[key] = build_program(NL, NS)
    nc, info = _CACHE[key]
    in_maps = []
    for c in range(n_cores):
        m = dict(shared)
        m["x"] = np.ascontiguousarray(x[c * NS:(c + 1) * NS])
        in_maps.append(m)
    res = run_bass_kernel_spmd(nc, in_maps, core_ids=list(range(n_cores)))
    out = np.concatenate([np.asarray(r["y"]) for r in res.results], axis=0)
    return out.astype(np.float32)
```
